# Optimizing a Trainium2 kernel written in Bass

```python
import math
import jax
import jax.numpy as jnp
from jax import lax
import numpy as np

D_MODEL = 1024
BATCH = 2
SEQ = 16384
DEPTH = 4

A_HEADS = 4
A_DK = 128
A_DV = 256
A_QK = A_HEADS * A_DK
A_V = A_HEADS * A_DV
GATE_SOFT_CAP = 15.0
B_HEADS = 8
B_DK = 128
B_DV = 128
B_WIDTH = B_HEADS * B_DK
CONV_K = 4
CHUNK = 64
D_FF = ((8 * D_MODEL // 3 + 255) // 256) * 256
DEEPNORM_ALPHA = (2.0 * DEPTH) ** 0.25
DEEPNORM_BETA = (8.0 * DEPTH) ** -0.25
LN_EPS = 1e-5
RMS_EPS = 1e-6
IN_SIZES = (A_QK, A_QK, A_V, A_V, A_HEADS, A_HEADS, 3 * B_WIDTH, B_WIDTH, B_HEADS, B_HEADS, D_MODEL, D_MODEL)
N_IN = sum(IN_SIZES)
SPLIT_POINTS = tuple(int(v) for v in np.cumsum(IN_SIZES)[:-1])

kernel_name = 'hybrid_mlstm_gdn_deepnorm'


def _layernorm(x, g, b):
    xf = x.astype(jnp.float32)
    mu = jnp.mean(xf, -1, keepdims=True)
    var = jnp.mean(jnp.square(xf - mu), -1, keepdims=True)
    y = (xf - mu) * lax.rsqrt(var + LN_EPS) * g.astype(jnp.float32) + b.astype(jnp.float32)
    return y.astype(x.dtype)


def _head_rmsnorm(h, g):
    return h * lax.rsqrt(jnp.mean(jnp.square(h), -1, keepdims=True) + RMS_EPS) * g.astype(jnp.float32)


def _soft_cap(x):
    return GATE_SOFT_CAP * jnp.tanh(x / GATE_SOFT_CAP)


def _to_chunks(t):
    bsz, s, h = t.shape[:3]
    t = t.reshape((bsz, s // CHUNK, CHUNK, h) + t.shape[3:])
    perm = (1, 0, 3, 2) + tuple(range(4, t.ndim))
    return t.transpose(perm)


def _from_chunks(t):
    nc, bsz, h, l, d = t.shape
    return t.transpose(1, 0, 3, 2, 4).reshape(bsz, nc * l, h, d)


def _causal_conv(u, w):
    k, c = w.shape
    return lax.conv_general_dilated(u, w.astype(u.dtype)[:, None, :], window_strides=(1,), padding=[(k - 1, 0)],
                                    dimension_numbers=('NWC', 'WIO', 'NWC'), feature_group_count=c)


def _mlstm_chunked(q, k, v, i_pre, f_pre):
    bsz, _, h, dk = q.shape
    dv = v.shape[-1]
    q = q * (dk ** -0.5)
    li = _soft_cap(i_pre)
    lf = jax.nn.log_sigmoid(_soft_cap(f_pre))
    xs = (_to_chunks(q), _to_chunks(k), _to_chunks(v), _to_chunks(li), _to_chunks(lf))
    causal = jnp.tril(jnp.ones((CHUNK, CHUNK), dtype=bool))

    def step(carry, chunk):
        c_st, n_st, m_st = carry
        qc, kc, vc, lic, lfc = chunk
        g = jnp.cumsum(lfc, -1)
        dmat = jnp.where(causal, g[..., :, None] - g[..., None, :] + lic[..., None, :], -jnp.inf)
        inter = g + m_st[..., None]
        m_t = jnp.maximum(inter, jnp.max(dmat, -1))
        w = jnp.exp(dmat - m_t[..., None])
        s = jnp.einsum('bhtd,bhsd->bhts', qc, kc) * w
        sc = jnp.exp(inter - m_t)
        num = jnp.einsum('bhts,bhsv->bhtv', s, vc) + sc[..., None] * jnp.einsum('bhtd,bhdv->bhtv', qc, c_st)
        den = jnp.sum(s, -1) + sc * jnp.einsum('bhtd,bhd->bht', qc, n_st)
        h_out = num / jnp.maximum(jnp.abs(den), jnp.exp(-m_t))[..., None]
        g_end = g[..., -1]
        dec = g_end[..., None] - g + lic
        m_new = jnp.maximum(g_end + m_st, jnp.max(dec, -1))
        wk = jnp.exp(dec - m_new[..., None])
        carry_scale = jnp.exp(g_end + m_st - m_new)
        c_new = carry_scale[..., None, None] * c_st + jnp.einsum('bhs,bhsd,bhsv->bhdv', wk, kc, vc)
        n_new = carry_scale[..., None] * n_st + jnp.einsum('bhs,bhsd->bhd', wk, kc)
        return (c_new, n_new, m_new), h_out

    init = (jnp.zeros((bsz, h, dk, dv), jnp.float32), jnp.zeros((bsz, h, dk), jnp.float32),
            jnp.zeros((bsz, h), jnp.float32))
    _, hs = lax.scan(step, init, xs)
    return _from_chunks(hs)


def _l2norm(t):
    return t * lax.rsqrt(jnp.sum(jnp.square(t), -1, keepdims=True) + RMS_EPS)


def _gated_delta_chunked(q, k, v, beta, g):
    bsz, _, h, dk = q.shape
    dv = v.shape[-1]
    q = _l2norm(q) * (dk ** -0.5)
    k = _l2norm(k)
    qc, kc, vc = _to_chunks(q), _to_chunks(k), _to_chunks(v)
    bc, gc = _to_chunks(beta), _to_chunks(g)
    gcum = jnp.cumsum(gc, -1)
    incl = jnp.tril(jnp.ones((CHUNK, CHUNK), dtype=bool))
    strict = jnp.tril(jnp.ones((CHUNK, CHUNK), dtype=bool), -1)
    diff = gcum[..., :, None] - gcum[..., None, :]
    decay_mat = jnp.where(incl, jnp.exp(jnp.where(incl, diff, 0.0)), 0.0)
    kk = jnp.einsum('nbhid,nbhjd->nbhij', kc, kc)
    a_low = jnp.where(strict, bc[..., :, None] * kk * decay_mat, 0.0)
    rhs = jnp.concatenate([vc * bc[..., None], kc * (bc * jnp.exp(gcum))[..., None]], -1)
    sol = lax.linalg.triangular_solve(jnp.eye(CHUNK, dtype=jnp.float32) + a_low, rhs, left_side=True, lower=True,
                                      transpose_a=False, conjugate_a=False, unit_diagonal=True)
    u, w = sol[..., :dv], sol[..., dv:]
    attn = jnp.einsum('nbhid,nbhjd->nbhij', qc, kc) * decay_mat
    qe = qc * jnp.exp(gcum)[..., None]
    kd = kc * jnp.exp(gcum[..., -1:] - gcum)[..., None]
    g_end = jnp.exp(gcum[..., -1])

    def step(s_st, chunk):
        uc, wc, ac, qec, kdc, gec = chunk
        v_new = uc - jnp.einsum('bhld,bhdv->bhlv', wc, s_st)
        o = jnp.einsum('bhld,bhdv->bhlv', qec, s_st) + jnp.einsum('bhij,bhjv->bhiv', ac, v_new)
        s_new = s_st * gec[..., None, None] + jnp.einsum('bhld,bhlv->bhdv', kdc, v_new)
        return s_new, o

    init = jnp.zeros((bsz, h, dk, dv), jnp.float32)
    _, os_ = lax.scan(step, init, (u, w, attn, qe, kd, g_end))
    return _from_chunks(os_)


def _layer(x, w_in, b_igate, b_fgate, g_mlstm_norm, conv_w, a_log, dt_bias, g_gdn_norm,
           w_branch_a, w_branch_b, w_out, ln1_g, ln1_b, w_ffn_up, w_ffn_down, ln2_g, ln2_b):
    bsz, s, _ = x.shape
    f32 = jnp.float32
    proj = jnp.einsum('bsd,dn->bsn', x, w_in)
    qa, ka, va, oa, ia, fa, qkv_b, zb, beta_b, a_b, gate_a, gate_b = jnp.split(proj, SPLIT_POINTS, axis=-1)

    ha = _mlstm_chunked(qa.reshape(bsz, s, A_HEADS, A_DK).astype(f32),
                        ka.reshape(bsz, s, A_HEADS, A_DK).astype(f32),
                        va.reshape(bsz, s, A_HEADS, A_DV).astype(f32),
                        ia.astype(f32) + b_igate.astype(f32),
                        fa.astype(f32) + b_fgate.astype(f32))
    ha = _head_rmsnorm(ha, g_mlstm_norm.reshape(A_HEADS, A_DV)).reshape(bsz, s, A_V)
    ha = (ha * jax.nn.sigmoid(oa.astype(f32))).astype(x.dtype)

    qkv = jax.nn.silu(_causal_conv(qkv_b, conv_w))
    qb, kb, vb = jnp.split(qkv, [B_WIDTH, 2 * B_WIDTH], axis=-1)
    beta = jax.nn.sigmoid(beta_b.astype(f32))
    g = -jnp.exp(a_log.astype(f32)) * jax.nn.softplus(a_b.astype(f32) + dt_bias.astype(f32))
    hb = _gated_delta_chunked(qb.reshape(bsz, s, B_HEADS, B_DK).astype(f32),
                              kb.reshape(bsz, s, B_HEADS, B_DK).astype(f32),
                              vb.reshape(bsz, s, B_HEADS, B_DV).astype(f32), beta, g)
    hb = _head_rmsnorm(hb, g_gdn_norm).reshape(bsz, s, B_WIDTH)
    hb = (hb * jax.nn.silu(zb.astype(f32))).astype(x.dtype)

    y = (jax.nn.sigmoid(gate_a) * jnp.einsum('bsc,cd->bsd', ha, w_branch_a)
         + jax.nn.sigmoid(gate_b) * jnp.einsum('bsc,cd->bsd', hb, w_branch_b))
    mix = jnp.einsum('bsd,de->bse', y, w_out)
    x = _layernorm(DEEPNORM_ALPHA * x + mix, ln1_g, ln1_b)

    gu = jnp.einsum('bsd,df->bsf', x, w_ffn_up)
    gt, up = jnp.split(gu, [D_FF], axis=-1)
    ffn = jnp.einsum('bsf,fd->bsd', jax.nn.silu(gt) * up, w_ffn_down)
    x = _layernorm(DEEPNORM_ALPHA * x + ffn, ln2_g, ln2_b)
    return x


def setup_inputs(seed: int = 0) -> dict:
    key = jax.random.key(seed)
    ks = jax.random.split(key, 20)
    f32 = jnp.float32

    def nrm(k, shape, scale):
        return jax.random.normal(k, shape, f32) * scale

    x = nrm(ks[0], (BATCH, SEQ, D_MODEL), 1.0)
    w_in = nrm(ks[1], (DEPTH, D_MODEL, N_IN), D_MODEL ** -0.5)
    b_igate = -2.0 + nrm(ks[2], (DEPTH, A_HEADS), 0.3)
    b_fgate = 3.0 + jax.random.uniform(ks[3], (DEPTH, A_HEADS), f32, 0.0, 3.0)
    g_mlstm_norm = 1.0 + nrm(ks[4], (DEPTH, A_V), 0.02)
    conv_w = nrm(ks[5], (DEPTH, CONV_K, 3 * B_WIDTH), CONV_K ** -0.5)
    a_log = jnp.log(jax.random.uniform(ks[6], (DEPTH, B_HEADS), f32, 1.0, 16.0))
    dt = jnp.exp(jax.random.uniform(ks[7], (DEPTH, B_HEADS), f32, math.log(1e-3), math.log(1e-1)))
    dt_bias = dt + jnp.log(-jnp.expm1(-dt))
    g_gdn_norm = 1.0 + nrm(ks[8], (DEPTH, B_DV), 0.02)
    w_branch_a = nrm(ks[9], (DEPTH, A_V, D_MODEL), A_V ** -0.5)
    w_branch_b = nrm(ks[10], (DEPTH, B_WIDTH, D_MODEL), B_WIDTH ** -0.5)
    w_out = nrm(ks[11], (DEPTH, D_MODEL, D_MODEL), D_MODEL ** -0.5 * DEEPNORM_BETA)
    ln1_g = 1.0 + nrm(ks[12], (DEPTH, D_MODEL), 0.02)
    ln1_b = nrm(ks[13], (DEPTH, D_MODEL), 0.02)
    w_ffn_up = nrm(ks[14], (DEPTH, D_MODEL, 2 * D_FF), D_MODEL ** -0.5)
    w_ffn_down = nrm(ks[15], (DEPTH, D_FF, D_MODEL), D_FF ** -0.5 * DEEPNORM_BETA)
    ln2_g = 1.0 + nrm(ks[16], (DEPTH, D_MODEL), 0.02)
    ln2_b = nrm(ks[17], (DEPTH, D_MODEL), 0.02)
    return {'x': x, 'w_in': w_in, 'b_igate': b_igate, 'b_fgate': b_fgate, 'g_mlstm_norm': g_mlstm_norm,
            'conv_w': conv_w, 'a_log': a_log, 'dt_bias': dt_bias, 'g_gdn_norm': g_gdn_norm,
            'w_branch_a': w_branch_a, 'w_branch_b': w_branch_b, 'w_out': w_out, 'ln1_g': ln1_g, 'ln1_b': ln1_b,
            'w_ffn_up': w_ffn_up, 'w_ffn_down': w_ffn_down, 'ln2_g': ln2_g, 'ln2_b': ln2_b}


def reference(x, w_in, b_igate, b_fgate, g_mlstm_norm, conv_w, a_log, dt_bias, g_gdn_norm,
              w_branch_a, w_branch_b, w_out, ln1_g, ln1_b, w_ffn_up, w_ffn_down, ln2_g, ln2_b):
    for l in range(DEPTH):
        x = _layer(x, w_in[l], b_igate[l], b_fgate[l], g_mlstm_norm[l], conv_w[l], a_log[l], dt_bias[l],
                   g_gdn_norm[l], w_branch_a[l], w_branch_b[l], w_out[l], ln1_g[l], ln1_b[l],
                   w_ffn_up[l], w_ffn_down[l], ln2_g[l], ln2_b[l])
    return x
```

```python
import numpy as np
from contextlib import ExitStack
import concourse.bass as bass
import concourse.mybir as mybir
from concourse.bass_utils import run_bass_kernel_spmd

F32 = mybir.dt.float32
ALU = mybir.AluOpType
AF = mybir.ActivationFunctionType
AX = mybir.AxisListType

P = 128
D = 1024
KC = 8
NIN = 9240
DFF = 2816
DEPTH = 4
SEQ = 16384
BATCH = 2
NCORES = 8
GRP = 4
ALPHA = (2.0 * DEPTH) ** 0.25
LN_EPS = 1e-5
RMS_EPS = 1e-6
CAP = 15.0
C_QA, C_KA, C_VA, C_OA, C_IF, C_QKVB, C_ZB, C_BG, C_GA, C_GB = 0, 512, 1024, 2048, 3072, 3080, 6152, 7176, 7192, 8216
R_GM, R_GG, R_L1G, R_L1B, R_L2G, R_L2B, R_GB, R_AL = 0, 1024, 1152, 2176, 3200, 4224, 5248, 5272
NROW = 5280

EPOCH = 12000
NDMA = 24
ENGS = ("pe", "act", "dve", "pool", "sp")
SAME_ENG_SYNC = True


class Buf:
    def __init__(self, name, t):
        self.name = name
        self.t = t
        self.lw = None
        self.rd = {}

    def __getitem__(self, k):
        return self.t[k]


class Sync:
    def __init__(self, nc, stack):
        self.nc = nc
        self.stack = stack
        self.streams = {e: [] for e in ENGS}
        self.cnt = {e: 0 for e in ENGS}
        self.sems = {e: [] for e in ENGS}
        self.known = {e: {} for e in ENGS}
        self.dma_sems = [stack.enter_context(nc.semaphore(f"dq{i}")) for i in range(NDMA)]
        self.dma_cnt = [0] * NDMA
        self.dma_rr_q = {}
        self.ncc = 0

    def _sem_for(self, eng, n):
        ep = (n - 1) // EPOCH
        while len(self.sems[eng]) <= ep:
            self.sems[eng].append(self.stack.enter_context(self.nc.semaphore(f"s_{eng}_{len(self.sems[eng])}")))
        return self.sems[eng][ep], (n - 1) % EPOCH + 1

    def _waits(self, eng, evs):
        out = []
        for ev in evs:
            if ev is None:
                continue
            key, val = ev[0], ev[1]
            if key == eng and (eng == "pe" or not SAME_ENG_SYNC):
                continue
            if self.known[eng].get(key, 0) >= val:
                continue
            self.known[eng][key] = val
            out.append((ev[2], ev[3]))
        return out

    def _collect(self, reads, writes):
        evs = []
        for b in reads:
            evs.append(b.lw)
        for b in writes:
            evs.append(b.lw)
            evs.extend(b.rd.values())
        return evs

    def op(self, eng, fn, reads=(), writes=()):
        waits = self._waits(eng, self._collect(reads, writes))
        self.cnt[eng] += 1
        n = self.cnt[eng]
        sem, sv = self._sem_for(eng, n)
        ev = (eng, n, sem, sv)
        self.streams[eng].append((waits, fn, sem, 1))
        for b in reads:
            b.rd[eng] = ev
        for b in writes:
            b.lw = ev
            b.rd = {}

    def dma(self, q, out_ap, in_ap, reads=(), writes=()):
        evs = self._collect(reads, writes)
        lo, hi = (0, NDMA - 4) if q != "pool" else (NDMA - 4, NDMA)
        rr = self.dma_rr_q.get(q, 0)
        i = lo + rr
        self.dma_rr_q[q] = (rr + 1) % (hi - lo)
        sem = self.dma_sems[i]
        prev = self.dma_cnt[i]
        if prev > 0:
            evs.append((("dma", i), prev, sem, prev))
        waits = self._waits(q, evs)
        self.dma_cnt[i] = prev + 16
        ev = (("dma", i), prev + 16, sem, prev + 16)
        self.streams[q].append((waits, lambda E: E.dma_start(out=out_ap, in_=in_ap), sem, 16))
        for b in reads:
            b.rd[("dma", i)] = ev
        for b in writes:
            b.lw = ev
            b.rd = {}

    def collective(self, kind, groups, in_ap, out_ap, reads=(), writes=()):
        evs = self._collect(reads, writes)
        waits = self._waits("pool", evs)
        sem = self.stack.enter_context(self.nc.semaphore(f"cc{self.ncc}"))
        key = ("cc", self.ncc)
        self.ncc += 1
        ev = (key, 1, sem, 1)

        def fn(E):
            return E.collective_compute(kind, ALU.bypass, replica_groups=groups, ins=[in_ap], outs=[out_ap])

        self.streams["pool"].append((waits, fn, sem, None))
        for b in reads:
            b.rd[key] = ev
        for b in writes:
            b.lw = ev
            b.rd = {}

    def final_wait(self, eng, bufs):
        evs = [b.lw for b in bufs]
        waits = self._waits(eng, evs)
        self.streams[eng].append((waits, None, None, 0))

    def emit(self):
        nc = self.nc
        streams = self.streams

        def mk(eng):
            def f(E):
                for waits, fn, sem, inc in streams[eng]:
                    for s, v in waits:
                        E.wait_ge(s, v)
                    if fn is None:
                        continue
                    ins = fn(E)
                    if inc is None:
                        ins.then_inc(sem)
                    else:
                        ins.then_inc(sem, inc)
            return f

        with nc.Block() as block:
            block.tensor(mk("pe"))
            block.scalar(mk("act"))
            block.vector(mk("dve"))
            block.gpsimd(mk("pool"))
            block.sync(mk("sp"))


class _Stop(Exception):
    pass


def build(NT, L):
    import os
    KSTOP = int(os.environ.get("KSTOP", "0"))

    def chk(n):
        if KSTOP == n:
            raise _Stop()

    T = NT * P
    nc = bass.Bass("TRN2", target_bir_lowering=False)
    stack = ExitStack()
    sy = Sync(nc, stack)

    def dram_in(name, shape):
        return nc.dram_tensor(name, shape, F32, kind="ExternalInput")

    x_d = dram_in("x", [T, D])
    w_in_d = dram_in("w_in", [L, D, NIN])
    w_g_d = dram_in("w_g", [L, D, P])
    w_ba_d = dram_in("w_ba", [L, D, D])
    w_bb_d = dram_in("w_bb", [L, D, D])
    w_out_d = dram_in("w_out", [L, D, D])
    w_up_d = dram_in("w_up", [L, D, 2 * DFF])
    w_dn_d = dram_in("w_dn", [L, DFF, D])
    rowp_d = dram_in("rowp", [L, P, NROW])
    convw_d = dram_in("convw", [L, P, 96])
    consts_d = dram_in("consts", [P, 4 * P])
    cmask_d = dram_in("cmask", [P, 12])
    y_d = nc.dram_tensor("y", [T, D], F32, kind="ExternalOutput")
    xb_d = [nc.dram_tensor(f"xb{i}", [T, D], F32) for i in range(2)]
    hin_d = [nc.dram_tensor(f"hin{l}", [P, D], F32) for l in range(L)]
    hout_d = [nc.dram_tensor(f"hout{l}", [NCORES * P, D], F32) for l in range(L)]
    sgi_d = [nc.dram_tensor(f"sgi{l}", [8 * P, 256], F32) for l in range(L)]
    sgo_d = [nc.dram_tensor(f"sgo{l}", [NCORES * 8 * P, 256], F32) for l in range(L)]
    smi_d = [nc.dram_tensor(f"smi{l}", [5 * P, 257], F32) for l in range(L)]
    smo_d = [nc.dram_tensor(f"smo{l}", [NCORES * 5 * P, 257], F32) for l in range(L)]
    groups = [list(range(NCORES))]
    CANDS = [0, 1, 2, 4, 5, 6]

    xin_b = [Buf(f"xin{i}", None) for i in range(NT)]
    xb_b = [[Buf(f"xb{j}_{i}", None) for i in range(NT)] for j in range(2)]
    y_b = [Buf(f"y{i}", None) for i in range(NT)]
    wbuf = Buf("weights", None)

    def sb(name, shape):
        return Buf(name, stack.enter_context(nc.sbuf_tensor("sb_" + name, shape, F32)))

    cst = sb("cst", [P, 4 * P])
    ident = cst.t[:, 0:P]
    ones = cst.t[:, P:2 * P]
    triU = cst.t[:, 2 * P:3 * P]
    triLs = cst.t[:, 3 * P:4 * P]
    cmask = sb("cmask", [P, 12])
    rp = sb("rp", [P, NROW])
    cw = sb("cw", [P, 24, 4])
    nexpA = sb("nexpA", [P, 8])
    xt = [sb(f"xt{i}", [P, D]) for i in range(2)]
    fmA = sb("fmA", [P, KC, P])
    fmB = sb("fmB", [P, KC, P])
    A = [sb(f"A{i}", [P, D]) for i in range(9)]
    cv = sb("cv", [P, 24, P])
    ubuf = sb("ubuf", [P, 24, P + 3])
    halo0 = sb("halo0", [P, 24, 3])
    kvtm = sb("kvtm", [P, 16, P])
    SX = sb("SX", [P, 8, 256])
    vaug = sb("vaug", [P, 4, 257])
    Caug = sb("Caug", [P, 4, 257])
    gseg = sb("gseg", [P, 4])
    Ccb = sb("Ccb", [P, 4, 257])
    eGt = sb("eGt", [P, 257])
    slabs = [sb(f"slab{i}", [P, 4096]) for i in range(2)]
    NSLOT = 4
    gw = []
    for s_ in range(NSLOT):
        d = {k: sb(f"g{s_}_{k}", [P, P]) for k in ("Lg", "Dgs", "DgT", "P0", "P1", "Q0", "Q1", "R0", "R1", "attnT", "kd")}
        d["bv"] = d["P0"]
        d["bkg"] = d["P1"]
        d["wT"] = d["Lg"]
        d["t2"] = d["Dgs"]
        d["u"] = sb(f"g{s_}_u", [P, 256])
        d["vn"] = sb(f"g{s_}_vn", [P, 256])
        gw.append(d)
    mw = [{k: sb(f"m{s}_{k}", [P, P]) for k in ("Lp", "DT", "W", "kw")} for s in range(2)]
    for s in range(2):
        mw[s]["num"] = sb(f"m{s}_num", [P, 257])
        mw[s]["tI"] = sb(f"m{s}_tI", [P, 257])
    sm = {k: sb(f"sm_{k}", [P, 24]) for k in ("raw", "gr", "t15", "e1", "l1", "tb", "az", "e2", "l2", "sp",
                                              "lfgg", "gc", "ge", "eg", "ege", "li", "wk", "ekd", "beta", "nbeta", "bg",
                                              "s1", "s2", "s3", "s4", "s5", "s6")}
    ps = [Buf(f"ps{i}", stack.enter_context(nc.psum_tensor(f"ps{i}", [P, 512], F32))) for i in range(8)]
    ps_rr = [0]

    def pget():
        b = ps[ps_rr[0]]
        ps_rr[0] = (ps_rr[0] + 1) % 8
        return b

    slab_rr = [0]

    def sget():
        b = slabs[slab_rr[0]]
        slab_rr[0] = (slab_rr[0] + 1) % len(slabs)
        return b

    def mm(out_ap, lhsT, rhs, start, stop, reads, writes):
        sy.op("pe", lambda E: E.matmul(out_ap, lhsT, rhs, start=start, stop=stop), reads, writes)

    def tr(out_ap, in_ap, reads, writes):
        sy.op("pe", lambda E: E.transpose(out_ap, in_ap, ident), list(reads) + [cst], writes)

    def act(out_ap, in_ap, func, reads, writes, bias=None, scale=None):
        kw = {}
        if bias is not None:
            kw["bias"] = bias
        if scale is not None:
            kw["scale"] = scale
        sy.op("act", lambda E: E.activation(out_ap, in_ap, func, **kw), reads, writes)

    def acopy(out_ap, in_ap, reads, writes):
        sy.op("act", lambda E: E.copy(out_ap, in_ap), reads, writes)

    def amul(out_ap, in_ap, m, reads, writes):
        sy.op("act", lambda E: E.mul(out_ap, in_ap, m), reads, writes)

    def tt(eng, out_ap, a, b, op, reads, writes):
        sy.op(eng, lambda E: E.tensor_tensor(out_ap, a, b, op), reads, writes)

    def ts(eng, out_ap, a, s1, s2, op0, op1, reads, writes):
        if op1 is None:
            sy.op(eng, lambda E: E.tensor_scalar(out_ap, a, s1, None, op0), reads, writes)
        else:
            sy.op(eng, lambda E: E.tensor_scalar(out_ap, a, s1, s2, op0, op1), reads, writes)

    def stt(eng, out_ap, a, s, b, op0, op1, reads, writes):
        eng = "dve"
        sy.op(eng, lambda E: E.scalar_tensor_tensor(out_ap, a, s, b, op0, op1), reads, writes)

    def cp(eng, out_ap, in_ap, reads, writes):
        sy.op(eng, lambda E: E.tensor_copy(out_ap, in_ap), reads, writes)

    def mset(eng, out_ap, v, writes):
        sy.op(eng, lambda E: E.memset(out_ap, v), (), writes)

    def load_slab(w_ap_2d, ncols, nk=KC):
        s = sget()
        dst = s.t[:, 0:nk * ncols].rearrange("p (k n) -> p k n", k=nk)
        src = w_ap_2d.rearrange("(k p) n -> p k n", p=P)
        sy.dma("sp", dst, src, reads=[wbuf], writes=[s])
        return s, dst

    def dense_tm(actT, w_ap_2d, ncols):
        s, sv = load_slab(w_ap_2d, ncols)
        pb = pget()
        for k in range(KC):
            mm(pb.t[:, 0:ncols], actT.t[:, k, :], sv[:, k, :], k == 0, k == KC - 1, [actT, s], [pb])
        return pb

    def dense_fm(actT, w_ap_2d, ncols, ntok=P):
        s, sv = load_slab(w_ap_2d, ncols)
        pb = pget()
        for cb in range(ncols // P):
            for k in range(KC):
                mm(pb.t[:, cb * P:cb * P + ntok], sv[:, k, cb * P:(cb + 1) * P], actT.t[:, k, 0:ntok], k == 0, k == KC - 1,
                   [actT, s], [pb])
        return pb

    def transpose_tm(src, dst):
        for half in range(2):
            pb = pget()
            for j in range(4):
                k = half * 4 + j
                tr(pb.t[:, j * P:(j + 1) * P], src.t[:, k * P:(k + 1) * P], [src], [pb])
            if half == 0:
                acopy(dst.t[:, 0:4, :], pb.t[:, :].rearrange("p (k n) -> p k n", k=4), [pb], [dst])
            else:
                cp("dve", dst.t[:, 4:8, :], pb.t[:, :].rearrange("p (k n) -> p k n", k=4), [pb], [dst])

    def rsqrt_op(out_ap, in_ap, mult, add, reads, writes):
        act(out_ap, in_ap, AF.Ln, reads, writes, bias=float(add), scale=float(mult))
        act(out_ap, out_ap, AF.Exp, writes, writes, scale=-0.5)

    def sumsq_rows(src_ap, scratch, out_ap, reads, writes, n):
        sy.op("act", lambda E: E.square(scratch.t[:, 0:n], src_ap), reads, [scratch])
        sy.op("dve", lambda E: E.reduce_sum(out_ap, scratch.t[:, 0:n], AX.X), [scratch], writes)

    def layernorm(z, gcol, bcol, out, scratch):
        s1, s2, s3 = sm["s1"], sm["s2"], sm["s3"]
        sy.op("dve", lambda E: E.reduce_sum(s1.t[:, 0:1], z.t[:, :], AX.X), [z], [s1])
        ts("dve", s1.t[:, 0:1], s1.t[:, 0:1], 1.0 / D, None, ALU.mult, None, [s1], [s1])
        ts("dve", z.t[:, :], z.t[:, :], s1.t[:, 0:1], None, ALU.subtract, None, [z, s1], [z])
        sumsq_rows(z.t[:, :], scratch, s2.t[:, 0:1], [z], [s2], D)
        rsqrt_op(s3.t[:, 0:1], s2.t[:, 0:1], 1.0 / D, LN_EPS, [s2], [s3])
        stt("dve", out.t[:, :], z.t[:, :], s3.t[:, 0:1], rp.t[:, gcol:gcol + D], ALU.mult, ALU.mult, [z, s3, rp], [out])
        tt("pool", out.t[:, :], out.t[:, :], rp.t[:, bcol:bcol + D], ALU.add, [out, rp], [out])

    cdram = Buf("cdram", None)
    sy.dma("sp", cst.t[:, :], consts_d[:, :], reads=[cdram], writes=[cst])
    sy.dma("sp", cmask.t[:, :], cmask_d[:, :], reads=[cdram], writes=[cmask])

    HM = 4
    HG = 8

    def gates(pass_):
        raw, gr, t15, e1, l1, tb, az, e2, l2, sp_, lfgg = (sm[k] for k in
                                                          ("raw", "gr", "t15", "e1", "l1", "tb", "az", "e2", "l2", "sp", "lfgg"))
        tt("dve", gr.t[:, :], raw.t[:, :], rp.t[:, R_GB:R_GB + 24], ALU.add, [raw, rp], [gr])
        act(t15.t[:, 0:8], gr.t[:, 0:8], AF.Tanh, [gr], [t15], scale=1.0 / CAP)
        act(tb.t[:, 0:8], gr.t[:, 8:16], AF.Tanh, [gr], [tb], scale=0.5)
        ts("dve", sm["li"].t[:, 0:4], t15.t[:, 0:4], CAP, None, ALU.mult, None, [t15], [sm["li"]])
        ts("dve", sm["beta"].t[:, 0:8], tb.t[:, 0:8], 0.5, 0.5, ALU.mult, ALU.add, [tb], [sm["beta"]])
        ts("dve", sm["nbeta"].t[:, 0:8], tb.t[:, 0:8], -0.5, -0.5, ALU.mult, ALU.add, [tb], [sm["nbeta"]])
        act(e1.t[:, 0:4], t15.t[:, 4:8], AF.Exp, [t15], [e1], scale=-CAP)
        ts("dve", az.t[:, 0:8], gr.t[:, 16:24], -1.0, None, ALU.mult, None, [gr], [az])
        tt("dve", az.t[:, 0:8], az.t[:, 0:8], gr.t[:, 16:24], ALU.max, [az, gr], [az])
        act(e1.t[:, 4:12], az.t[:, 0:8], AF.Exp, [az], [e1], scale=-1.0)
        ts("dve", e1.t[:, 0:12], e1.t[:, 0:12], 1.0, None, ALU.add, None, [e1], [e1])
        act(l1.t[:, 0:12], e1.t[:, 0:12], AF.Ln, [e1], [l1])
        ts("dve", lfgg.t[:, 0:4], l1.t[:, 0:4], -1.0, None, ALU.mult, None, [l1], [lfgg])
        stt("dve", sp_.t[:, 0:8], gr.t[:, 16:24], 0.0, l1.t[:, 4:12], ALU.max, ALU.add, [gr, l1], [sp_])
        tt("dve", lfgg.t[:, 4:12], sp_.t[:, 0:8], nexpA.t[:, 0:8], ALU.mult, [sp_, nexpA], [lfgg])
        pb = pget()
        mm(pb.t[:, 0:12], triU, lfgg.t[:, 0:12], True, True, [cst, lfgg], [pb])
        mm(pb.t[:, 16:28], ones, lfgg.t[:, 0:12], True, True, [cst, lfgg], [pb])
        gc, ge = sm["gc"], sm["ge"]
        cp("dve", gc.t[:, 0:12], pb.t[:, 0:12], [pb], [gc])
        cp("dve", ge.t[:, 0:12], pb.t[:, 16:28], [pb], [ge])
        act(sm["eg"].t[:, 0:12], gc.t[:, 0:12], AF.Exp, [gc], [sm["eg"]])
        act(sm["ege"].t[:, 0:12], ge.t[:, 0:12], AF.Exp, [ge], [sm["ege"]])
        s4 = sm["s4"]
        tt("dve", s4.t[:, 0:12], ge.t[:, 0:12], gc.t[:, 0:12], ALU.subtract, [ge, gc], [s4])
        act(sm["ekd"].t[:, 0:8], s4.t[:, 4:12], AF.Exp, [s4], [sm["ekd"]])
        tt("dve", s4.t[:, 0:4], s4.t[:, 0:4], sm["li"].t[:, 0:4], ALU.add, [s4, sm["li"]], [s4])
        act(sm["wk"].t[:, 0:4], s4.t[:, 0:4], AF.Exp, [s4], [sm["wk"]])
        tt("dve", sm["bg"].t[:, 0:8], sm["beta"].t[:, 0:8], sm["eg"].t[:, 4:12], ALU.mult, [sm["beta"], sm["eg"]], [sm["bg"]])
        if pass_ == 1:
            tt("dve", gseg.t[:, 0:4], gseg.t[:, 0:4], ge.t[:, 0:4], ALU.add, [gseg, ge], [gseg])

    def inproj(l, xT, pass_, halo_only=False):
        w = w_in_d[l]
        qk, og, zs, sga, sgb = A[0], A[1], A[2], A[3], A[4]

        def wap(c0, n):
            return w[:, c0:c0 + n]

        for k in range(6):
            pb = dense_fm(xT, wap(C_QKVB + 512 * k, 512), 512)
            eng = "act" if k % 2 == 0 else "dve"
            src = pb.t[:, :].rearrange("p (c n) -> p c n", c=4)
            if eng == "act":
                acopy(ubuf.t[:, 4 * k:4 * k + 4, 3:3 + P], src, [pb], [ubuf])
            else:
                cp("dve", ubuf.t[:, 4 * k:4 * k + 4, 3:3 + P], src, [pb], [ubuf])
        if halo_only:
            return
        pb = dense_tm(xT, w_g_d[l], P)
        cp("dve", sm["raw"].t[:, 0:24], pb.t[:, 0:24], [pb], [sm["raw"]])
        pb = dense_tm(xT, wap(C_KA, 512), 512)
        acopy(qk.t[:, 512:1024], pb.t[:, :], [pb], [qk])
        if pass_ == 2:
            pb = dense_tm(xT, wap(C_QA, 512), 512)
            acopy(qk.t[:, 0:512], pb.t[:, :], [pb], [qk])
        for k in range(2):
            pb = dense_tm(xT, wap(C_VA + 512 * k, 512), 512)
            cp("dve", vaug.t[:, 2 * k:2 * k + 2, 0:256], pb.t[:, :].rearrange("p (h n) -> p h n", h=2), [pb], [vaug])
        if pass_ == 2:
            for k in range(2):
                sl = slice(512 * k, 512 * (k + 1))
                pb = dense_tm(xT, wap(C_OA + 512 * k, 512), 512)
                act(og.t[:, sl], pb.t[:, :], AF.Tanh, [pb], [og], scale=0.5)
                pb = dense_tm(xT, wap(C_ZB + 512 * k, 512), 512)
                act(zs.t[:, sl], pb.t[:, :], AF.Tanh, [pb], [zs], scale=0.5)
                ts("dve", zs.t[:, sl], zs.t[:, sl], 0.5, 0.5, ALU.mult, ALU.add, [zs], [zs])
                tt("dve", zs.t[:, sl], zs.t[:, sl], pb.t[:, :], ALU.mult, [zs, pb], [zs])
                pb = dense_tm(xT, wap(C_GA + 512 * k, 512), 512)
                act(sga.t[:, sl], pb.t[:, :], AF.Tanh, [pb], [sga], scale=0.5)
                pb = dense_tm(xT, wap(C_GB + 512 * k, 512), 512)
                act(sgb.t[:, sl], pb.t[:, :], AF.Tanh, [pb], [sgb], scale=0.5)
            ts("pool", og.t[:, :], og.t[:, :], 0.5, 0.5, ALU.mult, ALU.add, [og], [og])
            ts("pool", sga.t[:, :], sga.t[:, :], 0.5, 0.5, ALU.mult, ALU.add, [sga], [sga])
            ts("pool", sgb.t[:, :], sgb.t[:, :], 0.5, 0.5, ALU.mult, ALU.add, [sgb], [sgb])

    def mlstm(pass_):
        qk, og, ha, qkT = A[0], A[1], A[5], A[7]
        li, lf = sm["li"], sm["lfgg"]
        if pass_ == 2:
            for half in range(2):
                pb = pget()
                for j in range(4):
                    c = half * 4 + j
                    tr(pb.t[:, j * P:(j + 1) * P], qk.t[:, c * P:(c + 1) * P], [qk], [pb])
                if half == 0:
                    amul(qkT.t[:, 0:512], pb.t[:, :], float(P ** -0.5), [pb], [qkT])
                else:
                    cp("dve", qkT.t[:, 512:1024], pb.t[:, :], [pb], [qkT])
        for h in range(HM):
            m = mw[h % 2]
            kTM = qk.t[:, 512 + h * P:512 + (h + 1) * P]
            if pass_ == 2:
                qT = qkT.t[:, h * P:(h + 1) * P]
                kT = qkT.t[:, 512 + h * P:512 + (h + 1) * P]
                Lp, DT, W, num, tI = m["Lp"], m["DT"], m["W"], m["num"], m["tI"]
                ts("pool", Lp.t[:, :], triLs, lf.t[:, h:h + 1], None, ALU.mult, None, [cst, lf], [Lp])
                stt("pool", Lp.t[:, :], ident, li.t[:, h:h + 1], Lp.t[:, :], ALU.mult, ALU.add, [cst, li, Lp], [Lp])
                pb = pget()
                mm(pb.t[:, 0:P], Lp.t[:, :], triU, True, True, [Lp, cst], [pb])
                mm(pb.t[:, P:2 * P], kT, qT, True, True, [qkT], [pb])
                act(DT.t[:, :], pb.t[:, 0:P], AF.Exp, [pb], [DT])
                tt("pool", DT.t[:, :], DT.t[:, :], triU, ALU.mult, [DT, cst], [DT])
                tt("dve", W.t[:, :], pb.t[:, P:2 * P], DT.t[:, :], ALU.mult, [pb, DT], [W])
                pa = pget()
                mm(pa.t[:, 0:257], W.t[:, :], vaug.t[:, h, :], True, True, [W, vaug], [pa])
                pc = pget()
                mm(pc.t[:, 0:257], qT, Caug.t[:, h, :], True, True, [qkT, Caug], [pc])
                amul(tI.t[:, :], pc.t[:, 0:257], sm["eg"].t[:, h:h + 1], [pc, sm["eg"]], [tI])
                tt("dve", num.t[:, :], pa.t[:, 0:257], tI.t[:, :], ALU.add, [pa, tI], [num])
                s1, s2, s3 = sm["s1"], sm["s2"], sm["s3"]
                ts("dve", s1.t[:, h:h + 1], num.t[:, 256:257], -1.0, None, ALU.mult, None, [num], [s1])
                tt("dve", s1.t[:, h:h + 1], s1.t[:, h:h + 1], num.t[:, 256:257], ALU.max, [s1, num], [s1])
                ts("dve", s1.t[:, h:h + 1], s1.t[:, h:h + 1], 1.0, None, ALU.max, None, [s1], [s1])
                sy.op("dve", lambda E, a=s1.t[:, h:h + 1]: E.reciprocal(a, a), [s1], [s1])
                sumsq_rows(num.t[:, 0:256], A[8], s2.t[:, h:h + 1], [num], [s2], 256)
                tt("dve", s3.t[:, h:h + 1], s1.t[:, h:h + 1], s1.t[:, h:h + 1], ALU.mult, [s1], [s3])
                stt("dve", s3.t[:, h:h + 1], s3.t[:, h:h + 1], 1.0 / 256, s2.t[:, h:h + 1], ALU.mult, ALU.mult, [s3, s2], [s3])
                rsqrt_op(s3.t[:, h:h + 1], s3.t[:, h:h + 1], 1.0, RMS_EPS, [s3], [s3])
                tt("dve", s3.t[:, h:h + 1], s3.t[:, h:h + 1], s1.t[:, h:h + 1], ALU.mult, [s3, s1], [s3])
                stt("dve", ha.t[:, h * 256:(h + 1) * 256], num.t[:, 0:256], s3.t[:, h:h + 1],
                    rp.t[:, R_GM + h * 256:R_GM + (h + 1) * 256], ALU.mult, ALU.mult, [num, s3, rp], [ha])
            kw_ = m["kw"]
            ts("pool", kw_.t[:, :], kTM, sm["wk"].t[:, h:h + 1], None, ALU.mult, None, [qk, sm["wk"]], [kw_])
            pd = pget()
            mm(pd.t[:, 0:257], kw_.t[:, :], vaug.t[:, h, :], True, True, [kw_, vaug], [pd])
            stt("dve", Caug.t[:, h, :], Caug.t[:, h, :], sm["ege"].t[:, h:h + 1], pd.t[:, 0:257], ALU.mult, ALU.add,
                [Caug, sm["ege"], pd], [Caug])
        if pass_ == 2:
            tt("pool", ha.t[:, :], ha.t[:, :], og.t[:, :], ALU.mult, [ha, og], [ha])

    def conv_halo_shift():
        cp("pool", ubuf.t[:, :, 0:3], ubuf.t[:, :, P:P + 3], [ubuf], [ubuf])

    def gdn(pass_):
        zs, hb = A[2], A[6]
        scr, scr2 = A[8], A[7]
        for part in range(3):
            c0 = part * 8
            if pass_ == 1 and part == 0:
                continue
            dst = cv.t[:, c0:c0 + 8, :]
            wv = cw.t[:, c0:c0 + 8, :]
            eng = "dve" if part % 2 == 0 else "pool"
            s3d = scr.t[:, :].rearrange("p (c n) -> p c n", c=8)
            for c in range(c0, c0 + 8):
                for k in range(4):
                    src = ubuf.t[:, c, k:k + P]
                    wsc = cw.t[:, c, k:k + 1]
                    if k == 0:
                        ts("pool", cv.t[:, c, :], src, wsc, None, ALU.mult, None, [ubuf, cw], [cv])
                    else:
                        stt("dve", cv.t[:, c, :], src, wsc, cv.t[:, c, :], ALU.mult, ALU.add, [ubuf, cw, cv], [cv])
            act(s3d, dst, AF.Tanh, [cv], [scr], scale=0.5)
            ts(eng, s3d, s3d, 0.5, 0.5, ALU.mult, ALU.add, [scr], [scr])
            tt(eng, dst, dst, s3d, ALU.mult, [cv, scr], [cv])
            if part < 2:
                sy.op("act", lambda E, a=s3d, b=dst: E.square(a, b), [cv], [scr])
                for hf in range(2):
                    pb = pget()
                    mm(pb.t[:, :], ones, scr.t[:, hf * 512:(hf + 1) * 512], True, True, [cst, scr], [pb])
                    r2 = scr2.t[:, hf * 512:(hf + 1) * 512]
                    if part == 0:
                        rsqrt_op(r2, pb.t[:, :], float(P), float(P) * RMS_EPS, [pb], [scr2])
                    else:
                        rsqrt_op(r2, pb.t[:, :], 1.0, RMS_EPS, [pb], [scr2])
                tt("dve", dst, dst, scr2.t[:, :].rearrange("p (c n) -> p c n", c=8), ALU.mult, [cv, scr2], [cv])
        chk(20)
        conv_halo_shift()
        for g4 in range(4):
            pb = pget()
            for j in range(4):
                c = 8 + g4 * 4 + j
                tr(pb.t[:, j * P:(j + 1) * P], cv.t[:, c, :], [cv], [pb])
            src = pb.t[:, :].rearrange("p (c n) -> p c n", c=4)
            if g4 % 2 == 0:
                acopy(kvtm.t[:, g4 * 4:g4 * 4 + 4, :], src, [pb], [kvtm])
            else:
                cp("dve", kvtm.t[:, g4 * 4:g4 * 4 + 4, :], src, [pb], [kvtm])
        gg = sm["lfgg"]
        chk(21)
        if pass_ == 1:
            for w0 in range(0, HG, NSLOT):
                hs = list(range(w0, w0 + NSLOT))

                def W(h):
                    return gw[h % NSLOT]

                def reg(bank, h, n=P):
                    j = h - w0
                    return bank.t[:, j * n:(j + 1) * n]

                for h in hs:
                    ts("pool", W(h)["Lg"].t[:, :], triLs, gg.t[:, 4 + h:5 + h], None, ALU.mult, None, [cst, gg], [W(h)["Lg"]])
                bA, bD = pget(), pget()
                for h in hs:
                    kT = cv.t[:, 8 + h, :]
                    mm(reg(bA, h), triU, W(h)["Lg"].t[:, :], True, True, [cst, W(h)["Lg"]], [bA])
                    mm(reg(bD, h), kT, kT, True, True, [cv], [bD])
                if pass_ == 2:
                    bA2, bD2 = pget(), pget()
                    for h in hs:
                        kT = cv.t[:, 8 + h, :]
                        qT = cv.t[:, h, :]
                        mm(reg(bA2, h), W(h)["Lg"].t[:, :], triU, True, True, [W(h)["Lg"], cst], [bA2])
                        mm(reg(bD2, h), kT, qT, True, True, [cv], [bD2])
                for h in hs:
                    w = W(h)
                    act(w["Dgs"].t[:, :], reg(bA, h), AF.Exp, [bA], [w["Dgs"]])
                for h in hs:
                    w = W(h)
                    tt("pool", w["Dgs"].t[:, :], w["Dgs"].t[:, :], triLs, ALU.mult, [w["Dgs"], cst], [w["Dgs"]])
                for h in hs:
                    w = W(h)
                    stt("dve", w["Q0"].t[:, :], reg(bD, h), sm["nbeta"].t[:, h:h + 1], w["Dgs"].t[:, :], ALU.mult, ALU.mult,
                        [bD, sm["nbeta"], w["Dgs"]], [w["Q0"]])
                if pass_ == 2:
                    for h in hs:
                        w = W(h)
                        act(w["DgT"].t[:, :], reg(bA2, h), AF.Exp, [bA2], [w["DgT"]])
                    for h in hs:
                        w = W(h)
                        tt("pool", w["DgT"].t[:, :], w["DgT"].t[:, :], triU, ALU.mult, [w["DgT"], cst], [w["DgT"]])
                    for h in hs:
                        w = W(h)
                        tt("dve", w["attnT"].t[:, :], reg(bD2, h), w["DgT"].t[:, :], ALU.mult, [bD2, w["DgT"]], [w["attnT"]])
                chk(22)
                b2 = pget()
                for h in hs:
                    tr(reg(b2, h), W(h)["Q0"].t[:, :], [W(h)["Q0"]], [b2])
                for h in hs:
                    w = W(h)
                    cp("dve", w["P0"].t[:, :], reg(b2, h), [b2], [w["P0"]])
                    tt("dve", w["R0"].t[:, :], reg(b2, h), ident, ALU.add, [b2, cst], [w["R0"]])
                cur = 0
                for k in range(1, 7):
                    nxt = 1 - cur
                    Qc, Qn = f"Q{cur}", f"Q{nxt}"
                    Pc, Pn = f"P{cur}", f"P{nxt}"
                    Rc, Rn = f"R{cur}", f"R{nxt}"
                    bQ = pget()
                    for h in hs:
                        w = W(h)
                        mm(reg(bQ, h), w[Pc].t[:, :], w[Qc].t[:, :], True, True, [w[Pc], w[Qc]], [bQ])
                    if k < 6:
                        bP = pget()
                        for h in hs:
                            w = W(h)
                            mm(reg(bP, h), w[Qc].t[:, :], w[Pc].t[:, :], True, True, [w[Pc], w[Qc]], [bP])
                    for h in hs:
                        w = W(h)
                        acopy(w[Qn].t[:, :], reg(bQ, h), [bQ], [w[Qn]])
                    if k < 6:
                        for h in hs:
                            w = W(h)
                            cp("dve", w[Pn].t[:, :], reg(bP, h), [bP], [w[Pn]])
                    bR = pget()
                    for h in hs:
                        w = W(h)
                        mm(reg(bR, h), w[Qn].t[:, :], w[Rc].t[:, :], True, True, [w[Qn], w[Rc]], [bR])
                    for h in hs:
                        w = W(h)
                        tt("dve", w[Rn].t[:, :], w[Rc].t[:, :], reg(bR, h), ALU.add, [w[Rc], bR], [w[Rn]])
                    cur = nxt
                chk(23)
                TTk = f"R{cur}"
                for h in hs:
                    w = W(h)
                    kTM = kvtm.t[:, h, :]
                    vTM = kvtm.t[:, 8 + h, :]
                    ts("pool", w["bv"].t[:, :], vTM, sm["beta"].t[:, h:h + 1], None, ALU.mult, None, [kvtm, sm["beta"]], [w["bv"]])
                    ts("pool", w["bkg"].t[:, :], kTM, sm["bg"].t[:, h:h + 1], None, ALU.mult, None, [kvtm, sm["bg"]], [w["bkg"]])
                    ts("pool", w["kd"].t[:, :], kTM, sm["ekd"].t[:, h:h + 1], None, ALU.mult, None, [kvtm, sm["ekd"]], [w["kd"]])
                bU, bW = pget(), pget()
                for h in hs:
                    w = W(h)
                    mm(reg(bU, h), w[TTk].t[:, :], w["bv"].t[:, :], True, True, [w[TTk], w["bv"]], [bU])
                    mm(reg(bW, h), w["bkg"].t[:, :], w[TTk].t[:, :], True, True, [w[TTk], w["bkg"]], [bW])
                for h in hs:
                    w = W(h)
                    acopy(w["wT"].t[:, :], reg(bW, h), [bW], [w["wT"]])
                    cp("dve", w["u"].t[:, 0:P], reg(bU, h), [bU], [w["u"]])
                chk(24)
                if pass_ == 2:
                    b5, b5b = pget(), pget()
                    for h in hs:
                        w = W(h)
                        S = SX.t[:, h, 0:P]
                        mm(reg(b5, h), w["wT"].t[:, :], S, True, True, [w["wT"], SX], [b5])
                        mm(reg(b5b, h), cv.t[:, h, :], S, True, True, [cv, SX], [b5b])
                    for h in hs:
                        w = W(h)
                        tt("dve", w["vn"].t[:, 0:P], w["u"].t[:, 0:P], reg(b5, h), ALU.subtract, [w["u"], b5], [w["vn"]])
                        amul(w["t2"].t[:, :], reg(b5b, h), sm["eg"].t[:, 4 + h:5 + h], [b5b, sm["eg"]], [w["t2"]])
                    b6, b7 = pget(), pget()
                    for h in hs:
                        w = W(h)
                        mm(reg(b6, h), w["attnT"].t[:, :], w["vn"].t[:, 0:P], True, True, [w["attnT"], w["vn"]], [b6])
                        mm(reg(b7, h), w["kd"].t[:, :], w["vn"].t[:, 0:P], True, True, [w["kd"], w["vn"]], [b7])
                    for h in hs:
                        w = W(h)
                        S = SX.t[:, h, 0:P]
                        tt("dve", hb.t[:, h * P:(h + 1) * P], reg(b6, h), w["t2"].t[:, :], ALU.add, [b6, w["t2"]], [hb])
                        stt("dve", S, S, sm["ege"].t[:, 4 + h:5 + h], reg(b7, h), ALU.mult, ALU.add, [SX, sm["ege"], b7], [SX])
                else:
                    nb = (NSLOT + 1) // 2
                    b5 = [pget() for _ in range(nb)]
                    for h in hs:
                        w = W(h)
                        j = h - w0
                        mm(b5[j // 2].t[:, (j % 2) * 256:(j % 2 + 1) * 256], w["wT"].t[:, :], SX.t[:, h, :], True, True,
                           [w["wT"], SX], [b5[j // 2]])
                    for h in hs:
                        w = W(h)
                        j = h - w0
                        tt("dve", w["vn"].t[:, :], w["u"].t[:, :], b5[j // 2].t[:, (j % 2) * 256:(j % 2 + 1) * 256], ALU.subtract,
                           [w["u"], b5[j // 2]], [w["vn"]])
                    b7 = [pget() for _ in range(nb)]
                    for h in hs:
                        w = W(h)
                        j = h - w0
                        mm(b7[j // 2].t[:, (j % 2) * 256:(j % 2 + 1) * 256], w["kd"].t[:, :], w["vn"].t[:, :], True, True,
                           [w["kd"], w["vn"]], [b7[j // 2]])
                    for h in hs:
                        j = h - w0
                        X = SX.t[:, h, :]
                        stt("dve", X, X, sm["ege"].t[:, 4 + h:5 + h], b7[j // 2].t[:, (j % 2) * 256:(j % 2 + 1) * 256], ALU.mult, ALU.add,
                            [SX, sm["ege"], b7[j // 2]], [SX])
        else:
            for h in range(HG):
                w = gw[h % NSLOT]
                qT = cv.t[:, h, :]
                kT = cv.t[:, 8 + h, :]
                kTM = kvtm.t[:, h, :]
                vTM = kvtm.t[:, 8 + h, :]
                Lg, Dgs, DgT = w["Lg"], w["Dgs"], w["DgT"]
                ts("pool", Lg.t[:, :], triLs, gg.t[:, 4 + h:5 + h], None, ALU.mult, None, [cst, gg], [Lg])
                pb = pget()
                pbD = pget()
                mm(pb.t[:, 0:P], triU, Lg.t[:, :], True, True, [cst, Lg], [pb])
                mm(pbD.t[:, P:2 * P], kT, kT, True, True, [cv], [pbD])
                if pass_ == 2:
                    mm(pb.t[:, 2 * P:3 * P], Lg.t[:, :], triU, True, True, [Lg, cst], [pb])
                    mm(pbD.t[:, 3 * P:4 * P], kT, qT, True, True, [cv], [pbD])
                act(Dgs.t[:, :], pb.t[:, 0:P], AF.Exp, [pb], [Dgs])
                tt("pool", Dgs.t[:, :], Dgs.t[:, :], triLs, ALU.mult, [Dgs, cst], [Dgs])
                Q, Pm, R = [w["Q0"], w["Q1"]], [w["P0"], w["P1"]], [w["R0"], w["R1"]]
                stt("dve", Q[0].t[:, :], pbD.t[:, P:2 * P], sm["nbeta"].t[:, h:h + 1], Dgs.t[:, :], ALU.mult, ALU.mult,
                    [pbD, sm["nbeta"], Dgs], [Q[0]])
                if pass_ == 2:
                    act(DgT.t[:, :], pb.t[:, 2 * P:3 * P], AF.Exp, [pb], [DgT])
                    tt("pool", DgT.t[:, :], DgT.t[:, :], triU, ALU.mult, [DgT, cst], [DgT])
                    tt("dve", w["attnT"].t[:, :], pbD.t[:, 3 * P:4 * P], DgT.t[:, :], ALU.mult, [pbD, DgT], [w["attnT"]])
                chk(22)
                p2 = pget()
                tr(p2.t[:, 0:P], Q[0].t[:, :], [Q[0]], [p2])
                cp("dve", Pm[0].t[:, :], p2.t[:, 0:P], [p2], [Pm[0]])
                tt("dve", R[0].t[:, :], p2.t[:, 0:P], ident, ALU.add, [p2, cst], [R[0]])
                cur = 0
                for k in range(1, 7):
                    nxt = 1 - cur
                    p3 = pget()
                    mm(p3.t[:, 0:P], Pm[cur].t[:, :], Q[cur].t[:, :], True, True, [Pm[cur], Q[cur]], [p3])
                    acopy(Q[nxt].t[:, :], p3.t[:, 0:P], [p3], [Q[nxt]])
                    if k < 6:
                        p3b = pget()
                        mm(p3b.t[:, 0:P], Q[cur].t[:, :], Pm[cur].t[:, :], True, True, [Pm[cur], Q[cur]], [p3b])
                        cp("dve", Pm[nxt].t[:, :], p3b.t[:, 0:P], [p3b], [Pm[nxt]])
                    p3c = pget()
                    mm(p3c.t[:, 0:P], Q[nxt].t[:, :], R[cur].t[:, :], True, True, [Q[nxt], R[cur]], [p3c])
                    tt("dve", R[nxt].t[:, :], R[cur].t[:, :], p3c.t[:, 0:P], ALU.add, [R[cur], p3c], [R[nxt]])
                    cur = nxt
                chk(23)
                TT = R[cur]
                bv, bkg, kd = w["bv"], w["bkg"], w["kd"]
                ts("pool", bv.t[:, :], vTM, sm["beta"].t[:, h:h + 1], None, ALU.mult, None, [kvtm, sm["beta"]], [bv])
                ts("pool", bkg.t[:, :], kTM, sm["bg"].t[:, h:h + 1], None, ALU.mult, None, [kvtm, sm["bg"]], [bkg])
                ts("pool", kd.t[:, :], kTM, sm["ekd"].t[:, h:h + 1], None, ALU.mult, None, [kvtm, sm["ekd"]], [kd])
                p4 = pget()
                p4b = pget()
                mm(p4.t[:, 0:P], TT.t[:, :], bv.t[:, :], True, True, [TT, bv], [p4])
                mm(p4b.t[:, 0:P], bkg.t[:, :], TT.t[:, :], True, True, [TT, bkg], [p4b])
                u, wT, vn = w["u"], w["wT"], w["vn"]
                acopy(wT.t[:, :], p4b.t[:, 0:P], [p4b], [wT])
                chk(24)
                egh = sm["ege"].t[:, 4 + h:5 + h]
                if pass_ == 2:
                    S = SX.t[:, h, 0:P]
                    p5 = pget()
                    mm(p5.t[:, 0:P], wT.t[:, :], S, True, True, [wT, SX], [p5])
                    p5b = pget()
                    mm(p5b.t[:, 0:P], qT, S, True, True, [cv, SX], [p5b])
                    cp("dve", u.t[:, 0:P], p4.t[:, 0:P], [p4], [u])
                    tt("dve", vn.t[:, 0:P], u.t[:, 0:P], p5.t[:, 0:P], ALU.subtract, [u, p5], [vn])
                    amul(w["t2"].t[:, :], p5b.t[:, 0:P], sm["eg"].t[:, 4 + h:5 + h], [p5b, sm["eg"]], [w["t2"]])
                    mm(p5.t[:, 2 * P:3 * P], w["attnT"].t[:, :], vn.t[:, 0:P], True, True, [w["attnT"], vn], [p5])
                    mm(p5.t[:, 3 * P:4 * P], kd.t[:, :], vn.t[:, 0:P], True, True, [kd, vn], [p5])
                    tt("dve", hb.t[:, h * P:(h + 1) * P], p5.t[:, 2 * P:3 * P], w["t2"].t[:, :], ALU.add, [p5, w["t2"]], [hb])
                    stt("dve", S, S, egh, p5.t[:, 3 * P:4 * P], ALU.mult, ALU.add, [SX, sm["ege"], p5], [SX])
                else:
                    X = SX.t[:, h, :]
                    cp("dve", u.t[:, 0:P], p4.t[:, 0:P], [p4], [u])
                    p5 = pget()
                    mm(p5.t[:, 0:256], wT.t[:, :], X, True, True, [wT, SX], [p5])
                    tt("dve", vn.t[:, :], u.t[:, :], p5.t[:, 0:256], ALU.subtract, [u, p5], [vn])
                    mm(p5.t[:, 256:512], kd.t[:, :], vn.t[:, :], True, True, [kd, vn], [p5])
                    stt("dve", X, X, egh, p5.t[:, 256:512], ALU.mult, ALU.add, [SX, sm["ege"], p5], [SX])
        if pass_ == 2:
            s5, s6 = sm["s5"], sm["s6"]
            sy.op("act", lambda E: E.square(scr.t[:, :], hb.t[:, :]), [hb], [scr])
            for h in range(HG):
                sy.op("dve", lambda E, h=h: E.reduce_sum(s5.t[:, h:h + 1], scr.t[:, h * P:(h + 1) * P], AX.X), [scr], [s5])
            rsqrt_op(s6.t[:, 0:8], s5.t[:, 0:8], 1.0 / P, RMS_EPS, [s5], [s6])
            for h in range(HG):
                stt("pool" if h % 2 else "dve", hb.t[:, h * P:(h + 1) * P], hb.t[:, h * P:(h + 1) * P], s6.t[:, h:h + 1],
                    rp.t[:, R_GG:R_GG + P], ALU.mult, ALU.mult, [hb, s6, rp], [hb])
            tt("pool", hb.t[:, :], hb.t[:, :], zs.t[:, :], ALU.mult, [hb, zs], [hb])

    def merge_ffn(l, xtb, out_b, out_ap):
        y, z, z2 = A[0], A[1], A[2]
        sga, sgb, ha, hb = A[3], A[4], A[5], A[6]
        scr = A[8]
        transpose_tm(ha, fmA)
        transpose_tm(hb, fmB)
        for k in range(2):
            sl = slice(512 * k, 512 * (k + 1))
            pa = dense_tm(fmA, w_ba_d[l][:, sl], 512)
            pb = dense_tm(fmB, w_bb_d[l][:, sl], 512)
            tt("dve", y.t[:, sl], pa.t[:, :], sga.t[:, sl], ALU.mult, [pa, sga], [y])
            tt("dve", scr.t[:, sl], pb.t[:, :], sgb.t[:, sl], ALU.mult, [pb, sgb], [scr])
            tt("pool", y.t[:, sl], y.t[:, sl], scr.t[:, sl], ALU.add, [y, scr], [y])
        transpose_tm(y, fmA)
        for k in range(2):
            sl = slice(512 * k, 512 * (k + 1))
            pa = dense_tm(fmA, w_out_d[l][:, sl], 512)
            stt("dve", z.t[:, sl], xtb.t[:, sl], float(ALPHA), pa.t[:, :], ALU.mult, ALU.add, [xtb, pa], [z])
        layernorm(z, R_L1G, R_L1B, z, scr)
        transpose_tm(z, fmB)
        actb = cv
        nblk = DFF // P
        for k in range(6):
            nb = min(4, nblk - 4 * k)
            ncol = nb * P
            pg = dense_fm(fmB, w_up_d[l][:, 512 * k:512 * k + ncol], ncol)
            pu = dense_fm(fmB, w_up_d[l][:, DFF + 512 * k:DFF + 512 * k + ncol], ncol)
            sc = scr.t[:, 0:ncol]
            act(sc, pg.t[:, 0:ncol], AF.Tanh, [pg], [scr], scale=0.5)
            ts("pool", sc, sc, 0.5, 0.5, ALU.mult, ALU.add, [scr], [scr])
            tt("dve", sc, sc, pg.t[:, 0:ncol], ALU.mult, [scr, pg], [scr])
            tt("dve", actb.t[:, 4 * k:4 * k + nb, :], sc.rearrange("p (c n) -> p c n", c=nb),
               pu.t[:, 0:ncol].rearrange("p (c n) -> p c n", c=nb), ALU.mult, [scr, pu], [actb])
        pd = [pget(), pget()]
        for k in range(6):
            nk = min(4, nblk - 4 * k)
            s, sv = load_slab(w_dn_d[l][512 * k:512 * k + nk * P, :], D, nk=nk)
            for kk in range(nk):
                kf = 4 * k + kk
                for hf in range(2):
                    mm(pd[hf].t[:, :], actb.t[:, kf, :], sv[:, kk, hf * 512:(hf + 1) * 512], kf == 0, kf == nblk - 1,
                       [actb, s], [pd[hf]])
        for hf in range(2):
            sl = slice(512 * hf, 512 * (hf + 1))
            stt("dve", z2.t[:, sl], z.t[:, sl], float(ALPHA), pd[hf].t[:, :], ALU.mult, ALU.add, [z, pd[hf]], [z2])
        layernorm(z2, R_L2G, R_L2B, z2, scr)
        sy.dma("sp", out_ap, z2.t[:, :], reads=[z2], writes=[out_b])

    try:
        for l in range(L):
            if l == 0:
                xin_ap, xin_bufs = x_d, xin_b
            else:
                xin_ap, xin_bufs = xb_d[(l - 1) % 2], xb_b[(l - 1) % 2]
            if l == L - 1:
                xout_ap, xout_bufs = y_d, y_b
            else:
                xout_ap, xout_bufs = xb_d[l % 2], xb_b[l % 2]
            sy.dma("sp", rp.t[:, :], rowp_d[l], reads=[cdram], writes=[rp])
            sy.dma("sp", cw.t[:, :, :], convw_d[l].rearrange("p (c k) -> p c k", k=4), reads=[cdram], writes=[cw])
            act(nexpA.t[:, :], rp.t[:, R_AL:R_AL + 8], AF.Exp, [rp], [nexpA])
            ts("dve", nexpA.t[:, :], nexpA.t[:, :], -1.0, None, ALU.mult, None, [nexpA], [nexpA])
            chk(1)
            hin_b, hout_b = Buf(f"hin{l}", None), Buf(f"hout{l}", None)
            sy.dma("pool", hin_d[l][:, :], xin_ap[T - P:T, :], reads=[xin_bufs[NT - 1]], writes=[hin_b])
            sy.collective("AllGather", groups, hin_d[l].ap().opt(), hout_d[l].ap().opt(), reads=[hin_b], writes=[hout_b])
            hx, hc = xt[0], A[8]
            mset("pool", hx.t[:, :], 0.0, [hx])
            for ci, c in enumerate(CANDS):
                sy.dma("sp", hc.t[:, :], hout_d[l][c * P:(c + 1) * P, :], reads=[hout_b], writes=[hc])
                stt("dve", hx.t[:, :], hc.t[:, :], cmask.t[:, 6 + ci:7 + ci], hx.t[:, :], ALU.mult, ALU.add, [hc, cmask, hx], [hx])
            chk(2)
            transpose_tm(hx, fmA)
            chk(3)
            inproj(l, fmA, 1, halo_only=True)
            chk(4)
            conv_halo_shift()
            cp("pool", halo0.t[:, :, :], ubuf.t[:, :, 0:3], [ubuf], [halo0])
            mset("pool", Caug.t[:, :, :], 0.0, [Caug])
            mset("pool", gseg.t[:, :], 0.0, [gseg])
            mset("pool", vaug.t[:, :, 256:257], 1.0, [vaug])
            mset("pool", SX.t[:, :, 0:P], 0.0, [SX])
            for h in range(HG):
                cp("pool", SX.t[:, h, P:2 * P], ident, [cst], [SX])
            for s in range(NSLOT):
                mset("pool", gw[s]["u"].t[:, :], 0.0, [gw[s]["u"]])
            for i in range(NT):
                xtb = xt[i % 2]
                sy.dma("sp", xtb.t[:, :], xin_ap[i * P:(i + 1) * P, :], reads=[xin_bufs[i]], writes=[xtb])
                transpose_tm(xtb, fmA)
                inproj(l, fmA, 1)
                chk(5)
                gates(1)
                chk(6)
                mlstm(1)
                chk(7)
                gdn(1)
                chk(8)
            sgi_b, sgo_b, smi_b, smo_b = (Buf(n, None) for n in ("sgi", "sgo", "smi", "smo"))
            sy.dma("pool", sgi_d[l][:, :].rearrange("(h p) n -> p h n", p=P), SX.t[:, :, :], reads=[SX], writes=[sgi_b])
            sy.collective("AllGather", groups, sgi_d[l].ap().opt(), sgo_d[l].ap().opt(), reads=[sgi_b], writes=[sgo_b])
            eG = eGt
            mset("pool", eG.t[:, :], 0.0, [eG])
            act(eG.t[:, 0:4], gseg.t[:, 0:4], AF.Exp, [gseg], [eG])
            sy.dma("pool", smi_d[l][0:4 * P, :].rearrange("(h p) n -> p h n", p=P), Caug.t[:, :, :], reads=[Caug], writes=[smi_b])
            sy.dma("pool", smi_d[l][4 * P:5 * P, :], eG.t[:, :], reads=[eG], writes=[smi_b])
            sy.collective("AllGather", groups, smi_d[l].ap().opt(), smo_d[l].ap().opt(), reads=[smi_b], writes=[smo_b])
            chk(9)
            mset("pool", SX.t[:, :, 0:P], 0.0, [SX])
            mset("pool", Caug.t[:, :, :], 0.0, [Caug])
            for ci, c in enumerate(CANDS):
                mc = cmask.t[:, ci:ci + 1]
                for hf in range(2):
                    Xc = A[hf]
                    sy.dma("sp", Xc.t[:, :].rearrange("p (h n) -> p h n", h=4),
                           sgo_d[l][(c * 8 + hf * 4) * P:(c * 8 + hf * 4 + 4) * P, :].rearrange("(h p) n -> p h n", p=P),
                           reads=[sgo_b], writes=[Xc])
                    for hh in range(4):
                        h = hf * 4 + hh
                        Lc = Xc.t[:, hh * 256:hh * 256 + P]
                        Pc = Xc.t[:, hh * 256 + P:hh * 256 + 2 * P]
                        w = gw[h % NSLOT]
                        S = SX.t[:, h, 0:P]
                        pb = pget()
                        tr(pb.t[:, 0:P], Pc, [Xc], [pb])
                        acopy(w["P0"].t[:, :], pb.t[:, 0:P], [pb], [w["P0"]])
                        mm(pb.t[:, P:2 * P], w["P0"].t[:, :], S, True, True, [w["P0"], SX], [pb])
                        tt("dve", w["Q0"].t[:, :], pb.t[:, P:2 * P], Lc, ALU.add, [pb, Xc], [w["Q0"]])
                        tt("dve", w["Q0"].t[:, :], w["Q0"].t[:, :], S, ALU.subtract, [w["Q0"], SX], [w["Q0"]])
                        stt("dve", S, w["Q0"].t[:, :], mc, S, ALU.mult, ALU.add, [w["Q0"], cmask, SX], [SX])
                Cc, Ec = Ccb, sm["s2"]
                sy.dma("sp", Cc.t[:, :, :],
                       smo_d[l][c * 5 * P:(c * 5 + 4) * P, :].rearrange("(h p) n -> p h n", p=P), reads=[smo_b], writes=[Cc])
                sy.dma("sp", Ec.t[:, 0:4], smo_d[l][(c * 5 + 4) * P:(c * 5 + 5) * P, 0:4], reads=[smo_b], writes=[Ec])
                ts("dve", Ec.t[:, 0:4], Ec.t[:, 0:4], -1.0, None, ALU.add, None, [Ec], [Ec])
                ts("dve", Ec.t[:, 0:4], Ec.t[:, 0:4], mc, 1.0, ALU.mult, ALU.add, [Ec, cmask], [Ec])
                for h in range(HM):
                    ts("dve", Caug.t[:, h, :], Caug.t[:, h, :], Ec.t[:, h:h + 1], None, ALU.mult, None, [Caug, Ec], [Caug])
                    stt("dve", Caug.t[:, h, :], Cc.t[:, h, :], mc, Caug.t[:, h, :], ALU.mult, ALU.add,
                        [Cc, cmask, Caug], [Caug])
            cp("pool", ubuf.t[:, :, 0:3], halo0.t[:, :, :], [halo0], [ubuf])
            chk(10)
            for i in range(NT):
                xtb = xt[i % 2]
                sy.dma("sp", xtb.t[:, :], xin_ap[i * P:(i + 1) * P, :], reads=[xin_bufs[i]], writes=[xtb])
                transpose_tm(xtb, fmA)
                inproj(l, fmA, 2)
                gates(2)
                chk(11)
                mlstm(2)
                chk(12)
                gdn(2)
                chk(13)
                merge_ffn(l, xtb, xout_bufs[i], xout_ap[i * P:(i + 1) * P, :])
    except _Stop:
        pass
    sy.final_wait("sp", y_b)
    sy.emit()
    stack.close()
    return nc


def host_prep(inputs, L=DEPTH):
    f = np.float32
    rowp = np.zeros((L, NROW), f)
    rowp[:, R_GM:R_GM + 1024] = inputs["g_mlstm_norm"][:L]
    rowp[:, R_GG:R_GG + 128] = inputs["g_gdn_norm"][:L]
    rowp[:, R_L1G:R_L1G + 1024] = inputs["ln1_g"][:L]
    rowp[:, R_L1B:R_L1B + 1024] = inputs["ln1_b"][:L]
    rowp[:, R_L2G:R_L2G + 1024] = inputs["ln2_g"][:L]
    rowp[:, R_L2B:R_L2B + 1024] = inputs["ln2_b"][:L]
    rowp[:, R_GB:R_GB + 4] = inputs["b_igate"][:L]
    rowp[:, R_GB + 4:R_GB + 8] = inputs["b_fgate"][:L]
    rowp[:, R_GB + 16:R_GB + 24] = inputs["dt_bias"][:L]
    rowp[:, R_AL:R_AL + 8] = inputs["a_log"][:L]
    rowp = np.ascontiguousarray(np.broadcast_to(rowp[:, None, :], (L, P, NROW)))
    cwt = np.asarray(inputs["conv_w"][:L], f)
    convw = np.ascontiguousarray(cwt.transpose(0, 2, 1).reshape(L, 24, P, 4).transpose(0, 2, 1, 3).reshape(L, P, 96))
    idx = np.arange(P)
    ident = np.eye(P, dtype=f)
    ones = np.ones((P, P), f)
    triU = (idx[:, None] <= idx[None, :]).astype(f)
    triLs = (idx[:, None] > idx[None, :]).astype(f)
    consts = np.ascontiguousarray(np.concatenate([ident, ones, triU, triLs], axis=1))
    return rowp, convw, consts


def run(inputs, NT, L):
    T = NT * P
    x = np.asarray(inputs["x"], np.float32)
    rowp, convw, consts = host_prep(inputs, L)
    w_in_np = np.asarray(inputs["w_in"], np.float32)
    w_g = np.zeros((L, D, P), np.float32)
    w_g[:, :, 0:8] = w_in_np[:L, :, C_IF:C_IF + 8]
    w_g[:, :, 8:24] = w_in_np[:L, :, C_BG:C_BG + 16]
    shared = {
        "w_in": np.ascontiguousarray(inputs["w_in"][:L], dtype=np.float32),
        "w_g": w_g,
        "w_ba": np.ascontiguousarray(inputs["w_branch_a"][:L], dtype=np.float32),
        "w_bb": np.ascontiguousarray(inputs["w_branch_b"][:L], dtype=np.float32),
        "w_out": np.ascontiguousarray(inputs["w_out"][:L], dtype=np.float32),
        "w_up": np.ascontiguousarray(inputs["w_ffn_up"][:L], dtype=np.float32),
        "w_dn": np.ascontiguousarray(inputs["w_ffn_down"][:L], dtype=np.float32),
        "rowp": rowp, "convw": convw, "consts": consts,
    }
    in_maps = []
    for r in range(NCORES):
        b, j = r // GRP, r % GRP
        cm = np.zeros((P, 12), np.float32)
        for ci, c in enumerate([0, 1, 2, 4, 5, 6]):
            cm[:, ci] = 1.0 if (c // GRP == b and c % GRP < j) else 0.0
            cm[:, 6 + ci] = 1.0 if (c // GRP == b and c % GRP == j - 1) else 0.0
        m = dict(shared)
        m["x"] = np.ascontiguousarray(x[b, j * T:(j + 1) * T, :])
        m["cmask"] = cm
        in_maps.append(m)
    nc = build(NT, L)
    res = run_bass_kernel_spmd(nc, in_maps, core_ids=list(range(NCORES)))
    out = np.zeros((BATCH, GRP * T, D), np.float32)
    for r in range(NCORES):
        b, j = r // GRP, r % GRP
        out[b, j * T:(j + 1) * T, :] = res.results[r]["y"]
    return out


def kernel(**inputs):
    return run(inputs, SEQ // GRP // P, DEPTH)
```

```python
import numpy as np
from contextlib import ExitStack
import concourse.bass as bass
import concourse.mybir as mybir
from concourse.bass_utils import run_bass_kernel_spmd

F32 = mybir.dt.float32
ALU = mybir.AluOpType
AF = mybir.ActivationFunctionType
AX = mybir.AxisListType

P = 128
D = 1024
KC = 8
NIN = 9240
DFF = 2816
DEPTH = 4
SEQ = 16384
BATCH = 2
NCORES = 8
GRP = 4
ALPHA = (2.0 * DEPTH) ** 0.25
LN_EPS = 1e-5
RMS_EPS = 1e-6
CAP = 15.0
C_QA, C_KA, C_VA, C_OA, C_IF, C_QKVB, C_ZB, C_BG, C_GA, C_GB = 0, 512, 1024, 2048, 3072, 3080, 6152, 7176, 7192, 8216
R_GM, R_GG, R_L1G, R_L1B, R_L2G, R_L2B, R_GB, R_AL = 0, 1024, 1152, 2176, 3200, 4224, 5248, 5272
NROW = 5280

EPOCH = 12000
NDMA = 24
ENGS = ("pe", "act", "dve", "pool", "sp")
SAME_ENG_SYNC = True


class Buf:
    def __init__(self, name, t):
        self.name = name
        self.t = t
        self.lw = None
        self.rd = {}

    def __getitem__(self, k):
        return self.t[k]


class Sync:
    def __init__(self, nc, stack):
        self.nc = nc
        self.stack = stack
        self.streams = {e: [] for e in ENGS}
        self.cnt = {e: 0 for e in ENGS}
        self.sems = {e: [] for e in ENGS}
        self.known = {e: {} for e in ENGS}
        self.dma_sems = [stack.enter_context(nc.semaphore(f"dq{i}")) for i in range(NDMA)]
        self.dma_cnt = [0] * NDMA
        self.dma_rr_q = {}
        self.ncc = 0

    def _sem_for(self, eng, n):
        ep = (n - 1) // EPOCH
        while len(self.sems[eng]) <= ep:
            self.sems[eng].append(self.stack.enter_context(self.nc.semaphore(f"s_{eng}_{len(self.sems[eng])}")))
        return self.sems[eng][ep], (n - 1) % EPOCH + 1

    def _waits(self, eng, evs):
        out = []
        for ev in evs:
            if ev is None:
                continue
            key, val = ev[0], ev[1]
            if key == eng and (eng == "pe" or not SAME_ENG_SYNC):
                continue
            if self.known[eng].get(key, 0) >= val:
                continue
            self.known[eng][key] = val
            out.append((ev[2], ev[3]))
        return out

    def _collect(self, reads, writes):
        evs = []
        for b in reads:
            evs.append(b.lw)
        for b in writes:
            evs.append(b.lw)
            evs.extend(b.rd.values())
        return evs

    def op(self, eng, fn, reads=(), writes=()):
        waits = self._waits(eng, self._collect(reads, writes))
        self.cnt[eng] += 1
        n = self.cnt[eng]
        sem, sv = self._sem_for(eng, n)
        ev = (eng, n, sem, sv)
        self.streams[eng].append((waits, fn, sem, 1))
        for b in reads:
            b.rd[eng] = ev
        for b in writes:
            b.lw = ev
            b.rd = {}

    def dma(self, q, out_ap, in_ap, reads=(), writes=()):
        evs = self._collect(reads, writes)
        lo, hi = (0, NDMA - 4) if q != "pool" else (NDMA - 4, NDMA)
        rr = self.dma_rr_q.get(q, 0)
        i = lo + rr
        self.dma_rr_q[q] = (rr + 1) % (hi - lo)
        sem = self.dma_sems[i]
        prev = self.dma_cnt[i]
        if prev > 0:
            evs.append((("dma", i), prev, sem, prev))
        waits = self._waits(q, evs)
        self.dma_cnt[i] = prev + 16
        ev = (("dma", i), prev + 16, sem, prev + 16)
        self.streams[q].append((waits, lambda E: E.dma_start(out=out_ap, in_=in_ap), sem, 16))
        for b in reads:
            b.rd[("dma", i)] = ev
        for b in writes:
            b.lw = ev
            b.rd = {}

    def collective(self, kind, groups, in_ap, out_ap, reads=(), writes=()):
        evs = self._collect(reads, writes)
        waits = self._waits("pool", evs)
        sem = self.stack.enter_context(self.nc.semaphore(f"cc{self.ncc}"))
        key = ("cc", self.ncc)
        self.ncc += 1
        ev = (key, 1, sem, 1)

        def fn(E):
            return E.collective_compute(kind, ALU.bypass, replica_groups=groups, ins=[in_ap], outs=[out_ap])

        self.streams["pool"].append((waits, fn, sem, None))
        for b in reads:
            b.rd[key] = ev
        for b in writes:
            b.lw = ev
            b.rd = {}

    def final_wait(self, eng, bufs):
        evs = [b.lw for b in bufs]
        waits = self._waits(eng, evs)
        self.streams[eng].append((waits, None, None, 0))

    def emit(self):
        nc = self.nc
        streams = self.streams

        def mk(eng):
            def f(E):
                for waits, fn, sem, inc in streams[eng]:
                    for s, v in waits:
                        E.wait_ge(s, v)
                    if fn is None:
                        continue
                    ins = fn(E)
                    if inc is None:
                        ins.then_inc(sem)
                    else:
                        ins.then_inc(sem, inc)
            return f

        with nc.Block() as block:
            block.tensor(mk("pe"))
            block.scalar(mk("act"))
            block.vector(mk("dve"))
            block.gpsimd(mk("pool"))
            block.sync(mk("sp"))


class _Stop(Exception):
    pass


def build(NT, L):
    import os
    KSTOP = int(os.environ.get("KSTOP", "0"))

    def chk(n):
        if KSTOP == n:
            raise _Stop()

    T = NT * P
    nc = bass.Bass("TRN2", target_bir_lowering=False)
    stack = ExitStack()
    sy = Sync(nc, stack)

    def dram_in(name, shape):
        return nc.dram_tensor(name, shape, F32, kind="ExternalInput")

    x_d = dram_in("x", [T, D])
    w_in_d = dram_in("w_in", [L, D, NIN])
    w_g_d = dram_in("w_g", [L, D, P])
    w_ba_d = dram_in("w_ba", [L, D, D])
    w_bb_d = dram_in("w_bb", [L, D, D])
    w_out_d = dram_in("w_out", [L, D, D])
    w_up_d = dram_in("w_up", [L, D, 2 * DFF])
    w_dn_d = dram_in("w_dn", [L, DFF, D])
    rowp_d = dram_in("rowp", [L, P, NROW])
    convw_d = dram_in("convw", [L, P, 96])
    consts_d = dram_in("consts", [P, 4 * P])
    cmask_d = dram_in("cmask", [P, 12])
    y_d = nc.dram_tensor("y", [T, D], F32, kind="ExternalOutput")
    xb_d = [nc.dram_tensor(f"xb{i}", [T, D], F32) for i in range(2)]
    hin_d = [nc.dram_tensor(f"hin{l}", [P, D], F32) for l in range(L)]
    hout_d = [nc.dram_tensor(f"hout{l}", [NCORES * P, D], F32) for l in range(L)]
    sgi_d = [nc.dram_tensor(f"sgi{l}", [8 * P, 256], F32) for l in range(L)]
    sgo_d = [nc.dram_tensor(f"sgo{l}", [NCORES * 8 * P, 256], F32) for l in range(L)]
    smi_d = [nc.dram_tensor(f"smi{l}", [5 * P, 257], F32) for l in range(L)]
    smo_d = [nc.dram_tensor(f"smo{l}", [NCORES * 5 * P, 257], F32) for l in range(L)]
    groups = [list(range(NCORES))]
    CANDS = [0, 1, 2, 4, 5, 6]

    xin_b = [Buf(f"xin{i}", None) for i in range(NT)]
    xb_b = [[Buf(f"xb{j}_{i}", None) for i in range(NT)] for j in range(2)]
    y_b = [Buf(f"y{i}", None) for i in range(NT)]
    wbuf = Buf("weights", None)

    def sb(name, shape):
        return Buf(name, stack.enter_context(nc.sbuf_tensor("sb_" + name, shape, F32)))

    cst = sb("cst", [P, 4 * P])
    ident = cst.t[:, 0:P]
    ones = cst.t[:, P:2 * P]
    triU = cst.t[:, 2 * P:3 * P]
    triLs = cst.t[:, 3 * P:4 * P]
    cmask = sb("cmask", [P, 12])
    rp = sb("rp", [P, NROW])
    cw = sb("cw", [P, 24, 4])
    nexpA = sb("nexpA", [P, 8])
    xt = [sb(f"xt{i}", [P, D]) for i in range(2)]
    fmA = sb("fmA", [P, KC, P])
    fmB = sb("fmB", [P, KC, P])
    A = [sb(f"A{i}", [P, D]) for i in range(9)]
    cv = sb("cv", [P, 24, P])
    ubuf = sb("ubuf", [P, 24, P + 3])
    halo0 = sb("halo0", [P, 24, 3])
    kvtm = sb("kvtm", [P, 16, P])
    SX = sb("SX", [P, 8, 256])
    vaug = sb("vaug", [P, 4, 257])
    Caug = sb("Caug", [P, 4, 257])
    gseg = sb("gseg", [P, 4])
    Ccb = sb("Ccb", [P, 4, 257])
    eGt = sb("eGt", [P, 257])
    slabs = [sb(f"slab{i}", [P, 4096]) for i in range(3)]
    NSLOT = 2
    gw = [{k: sb(f"g{s}_{k}", [P, P]) for k in ("Lg", "Dgs", "DgT", "P0", "P1", "Q0", "Q1", "R0", "R1",
                                                 "bv", "bkg", "kd", "wT", "attnT")} for s in range(NSLOT)]
    for s in range(NSLOT):
        gw[s]["u"] = sb(f"g{s}_u", [P, 256])
        gw[s]["vn"] = sb(f"g{s}_vn", [P, 256])
        gw[s]["t2"] = sb(f"g{s}_t2", [P, P])
    mw = [{k: sb(f"m{s}_{k}", [P, P]) for k in ("Lp", "DT", "W", "kw")} for s in range(2)]
    for s in range(2):
        mw[s]["num"] = sb(f"m{s}_num", [P, 257])
        mw[s]["tI"] = sb(f"m{s}_tI", [P, 257])
    sm = {k: sb(f"sm_{k}", [P, 24]) for k in ("raw", "gr", "t15", "e1", "l1", "tb", "az", "e2", "l2", "sp",
                                              "lfgg", "gc", "ge", "eg", "ege", "li", "wk", "ekd", "beta", "nbeta", "bg",
                                              "s1", "s2", "s3", "s4", "s5", "s6")}
    ps = [Buf(f"ps{i}", stack.enter_context(nc.psum_tensor(f"ps{i}", [P, 512], F32))) for i in range(8)]
    ps_rr = [0]

    def pget():
        b = ps[ps_rr[0]]
        ps_rr[0] = (ps_rr[0] + 1) % 8
        return b

    slab_rr = [0]

    def sget():
        b = slabs[slab_rr[0]]
        slab_rr[0] = (slab_rr[0] + 1) % len(slabs)
        return b

    def mm(out_ap, lhsT, rhs, start, stop, reads, writes):
        sy.op("pe", lambda E: E.matmul(out_ap, lhsT, rhs, start=start, stop=stop), reads, writes)

    def tr(out_ap, in_ap, reads, writes):
        sy.op("pe", lambda E: E.transpose(out_ap, in_ap, ident), list(reads) + [cst], writes)

    def act(out_ap, in_ap, func, reads, writes, bias=None, scale=None):
        kw = {}
        if bias is not None:
            kw["bias"] = bias
        if scale is not None:
            kw["scale"] = scale
        sy.op("act", lambda E: E.activation(out_ap, in_ap, func, **kw), reads, writes)

    def acopy(out_ap, in_ap, reads, writes):
        sy.op("act", lambda E: E.copy(out_ap, in_ap), reads, writes)

    def amul(out_ap, in_ap, m, reads, writes):
        sy.op("act", lambda E: E.mul(out_ap, in_ap, m), reads, writes)

    def tt(eng, out_ap, a, b, op, reads, writes):
        sy.op(eng, lambda E: E.tensor_tensor(out_ap, a, b, op), reads, writes)

    def ts(eng, out_ap, a, s1, s2, op0, op1, reads, writes):
        if op1 is None:
            sy.op(eng, lambda E: E.tensor_scalar(out_ap, a, s1, None, op0), reads, writes)
        else:
            sy.op(eng, lambda E: E.tensor_scalar(out_ap, a, s1, s2, op0, op1), reads, writes)

    def stt(eng, out_ap, a, s, b, op0, op1, reads, writes):
        eng = "dve"
        sy.op(eng, lambda E: E.scalar_tensor_tensor(out_ap, a, s, b, op0, op1), reads, writes)

    def cp(eng, out_ap, in_ap, reads, writes):
        sy.op(eng, lambda E: E.tensor_copy(out_ap, in_ap), reads, writes)

    def mset(eng, out_ap, v, writes):
        sy.op(eng, lambda E: E.memset(out_ap, v), (), writes)

    def load_slab(w_ap_2d, ncols, nk=KC):
        s = sget()
        dst = s.t[:, 0:nk * ncols].rearrange("p (k n) -> p k n", k=nk)
        src = w_ap_2d.rearrange("(k p) n -> p k n", p=P)
        sy.dma("sp", dst, src, reads=[wbuf], writes=[s])
        return s, dst

    def dense_tm(actT, w_ap_2d, ncols):
        s, sv = load_slab(w_ap_2d, ncols)
        pb = pget()
        for k in range(KC):
            mm(pb.t[:, 0:ncols], actT.t[:, k, :], sv[:, k, :], k == 0, k == KC - 1, [actT, s], [pb])
        return pb

    def dense_fm(actT, w_ap_2d, ncols, ntok=P):
        s, sv = load_slab(w_ap_2d, ncols)
        pb = pget()
        for cb in range(ncols // P):
            for k in range(KC):
                mm(pb.t[:, cb * P:cb * P + ntok], sv[:, k, cb * P:(cb + 1) * P], actT.t[:, k, 0:ntok], k == 0, k == KC - 1,
                   [actT, s], [pb])
        return pb

    def transpose_tm(src, dst):
        for half in range(2):
            pb = pget()
            for j in range(4):
                k = half * 4 + j
                tr(pb.t[:, j * P:(j + 1) * P], src.t[:, k * P:(k + 1) * P], [src], [pb])
            if half == 0:
                acopy(dst.t[:, 0:4, :], pb.t[:, :].rearrange("p (k n) -> p k n", k=4), [pb], [dst])
            else:
                cp("dve", dst.t[:, 4:8, :], pb.t[:, :].rearrange("p (k n) -> p k n", k=4), [pb], [dst])

    def rsqrt_op(out_ap, in_ap, mult, add, reads, writes):
        act(out_ap, in_ap, AF.Ln, reads, writes, bias=float(add), scale=float(mult))
        act(out_ap, out_ap, AF.Exp, writes, writes, scale=-0.5)

    def sumsq_rows(src_ap, scratch, out_ap, reads, writes, n):
        sy.op("act", lambda E: E.square(scratch.t[:, 0:n], src_ap), reads, [scratch])
        sy.op("dve", lambda E: E.reduce_sum(out_ap, scratch.t[:, 0:n], AX.X), [scratch], writes)

    def layernorm(z, gcol, bcol, out, scratch):
        s1, s2, s3 = sm["s1"], sm["s2"], sm["s3"]
        sy.op("dve", lambda E: E.reduce_sum(s1.t[:, 0:1], z.t[:, :], AX.X), [z], [s1])
        ts("dve", s1.t[:, 0:1], s1.t[:, 0:1], 1.0 / D, None, ALU.mult, None, [s1], [s1])
        ts("dve", z.t[:, :], z.t[:, :], s1.t[:, 0:1], None, ALU.subtract, None, [z, s1], [z])
        sumsq_rows(z.t[:, :], scratch, s2.t[:, 0:1], [z], [s2], D)
        rsqrt_op(s3.t[:, 0:1], s2.t[:, 0:1], 1.0 / D, LN_EPS, [s2], [s3])
        stt("dve", out.t[:, :], z.t[:, :], s3.t[:, 0:1], rp.t[:, gcol:gcol + D], ALU.mult, ALU.mult, [z, s3, rp], [out])
        tt("pool", out.t[:, :], out.t[:, :], rp.t[:, bcol:bcol + D], ALU.add, [out, rp], [out])

    cdram = Buf("cdram", None)
    sy.dma("sp", cst.t[:, :], consts_d[:, :], reads=[cdram], writes=[cst])
    sy.dma("sp", cmask.t[:, :], cmask_d[:, :], reads=[cdram], writes=[cmask])

    HM = 4
    HG = 8

    def gates(pass_):
        raw, gr, t15, e1, l1, tb, az, e2, l2, sp_, lfgg = (sm[k] for k in
                                                          ("raw", "gr", "t15", "e1", "l1", "tb", "az", "e2", "l2", "sp", "lfgg"))
        tt("dve", gr.t[:, :], raw.t[:, :], rp.t[:, R_GB:R_GB + 24], ALU.add, [raw, rp], [gr])
        act(t15.t[:, 0:8], gr.t[:, 0:8], AF.Tanh, [gr], [t15], scale=1.0 / CAP)
        act(tb.t[:, 0:8], gr.t[:, 8:16], AF.Tanh, [gr], [tb], scale=0.5)
        ts("dve", sm["li"].t[:, 0:4], t15.t[:, 0:4], CAP, None, ALU.mult, None, [t15], [sm["li"]])
        ts("dve", sm["beta"].t[:, 0:8], tb.t[:, 0:8], 0.5, 0.5, ALU.mult, ALU.add, [tb], [sm["beta"]])
        ts("dve", sm["nbeta"].t[:, 0:8], tb.t[:, 0:8], -0.5, -0.5, ALU.mult, ALU.add, [tb], [sm["nbeta"]])
        act(e1.t[:, 0:4], t15.t[:, 4:8], AF.Exp, [t15], [e1], scale=-CAP)
        ts("dve", az.t[:, 0:8], gr.t[:, 16:24], -1.0, None, ALU.mult, None, [gr], [az])
        tt("dve", az.t[:, 0:8], az.t[:, 0:8], gr.t[:, 16:24], ALU.max, [az, gr], [az])
        act(e1.t[:, 4:12], az.t[:, 0:8], AF.Exp, [az], [e1], scale=-1.0)
        ts("dve", e1.t[:, 0:12], e1.t[:, 0:12], 1.0, None, ALU.add, None, [e1], [e1])
        act(l1.t[:, 0:12], e1.t[:, 0:12], AF.Ln, [e1], [l1])
        ts("dve", lfgg.t[:, 0:4], l1.t[:, 0:4], -1.0, None, ALU.mult, None, [l1], [lfgg])
        stt("dve", sp_.t[:, 0:8], gr.t[:, 16:24], 0.0, l1.t[:, 4:12], ALU.max, ALU.add, [gr, l1], [sp_])
        tt("dve", lfgg.t[:, 4:12], sp_.t[:, 0:8], nexpA.t[:, 0:8], ALU.mult, [sp_, nexpA], [lfgg])
        pb = pget()
        mm(pb.t[:, 0:12], triU, lfgg.t[:, 0:12], True, True, [cst, lfgg], [pb])
        mm(pb.t[:, 16:28], ones, lfgg.t[:, 0:12], True, True, [cst, lfgg], [pb])
        gc, ge = sm["gc"], sm["ge"]
        cp("dve", gc.t[:, 0:12], pb.t[:, 0:12], [pb], [gc])
        cp("dve", ge.t[:, 0:12], pb.t[:, 16:28], [pb], [ge])
        act(sm["eg"].t[:, 0:12], gc.t[:, 0:12], AF.Exp, [gc], [sm["eg"]])
        act(sm["ege"].t[:, 0:12], ge.t[:, 0:12], AF.Exp, [ge], [sm["ege"]])
        s4 = sm["s4"]
        tt("dve", s4.t[:, 0:12], ge.t[:, 0:12], gc.t[:, 0:12], ALU.subtract, [ge, gc], [s4])
        act(sm["ekd"].t[:, 0:8], s4.t[:, 4:12], AF.Exp, [s4], [sm["ekd"]])
        tt("dve", s4.t[:, 0:4], s4.t[:, 0:4], sm["li"].t[:, 0:4], ALU.add, [s4, sm["li"]], [s4])
        act(sm["wk"].t[:, 0:4], s4.t[:, 0:4], AF.Exp, [s4], [sm["wk"]])
        tt("dve", sm["bg"].t[:, 0:8], sm["beta"].t[:, 0:8], sm["eg"].t[:, 4:12], ALU.mult, [sm["beta"], sm["eg"]], [sm["bg"]])
        if pass_ == 1:
            tt("dve", gseg.t[:, 0:4], gseg.t[:, 0:4], ge.t[:, 0:4], ALU.add, [gseg, ge], [gseg])

    def inproj(l, xT, pass_, halo_only=False):
        w = w_in_d[l]
        qk, og, zs, sga, sgb = A[0], A[1], A[2], A[3], A[4]

        def wap(c0, n):
            return w[:, c0:c0 + n]

        for k in range(6):
            pb = dense_fm(xT, wap(C_QKVB + 512 * k, 512), 512)
            eng = "act" if k % 2 == 0 else "dve"
            src = pb.t[:, :].rearrange("p (c n) -> p c n", c=4)
            if eng == "act":
                acopy(ubuf.t[:, 4 * k:4 * k + 4, 3:3 + P], src, [pb], [ubuf])
            else:
                cp("dve", ubuf.t[:, 4 * k:4 * k + 4, 3:3 + P], src, [pb], [ubuf])
        if halo_only:
            return
        pb = dense_tm(xT, w_g_d[l], P)
        cp("dve", sm["raw"].t[:, 0:24], pb.t[:, 0:24], [pb], [sm["raw"]])
        pb = dense_tm(xT, wap(C_KA, 512), 512)
        acopy(qk.t[:, 512:1024], pb.t[:, :], [pb], [qk])
        if pass_ == 2:
            pb = dense_tm(xT, wap(C_QA, 512), 512)
            acopy(qk.t[:, 0:512], pb.t[:, :], [pb], [qk])
        for k in range(2):
            pb = dense_tm(xT, wap(C_VA + 512 * k, 512), 512)
            cp("dve", vaug.t[:, 2 * k:2 * k + 2, 0:256], pb.t[:, :].rearrange("p (h n) -> p h n", h=2), [pb], [vaug])
        if pass_ == 2:
            for k in range(2):
                sl = slice(512 * k, 512 * (k + 1))
                pb = dense_tm(xT, wap(C_OA + 512 * k, 512), 512)
                act(og.t[:, sl], pb.t[:, :], AF.Tanh, [pb], [og], scale=0.5)
                pb = dense_tm(xT, wap(C_ZB + 512 * k, 512), 512)
                act(zs.t[:, sl], pb.t[:, :], AF.Tanh, [pb], [zs], scale=0.5)
                ts("dve", zs.t[:, sl], zs.t[:, sl], 0.5, 0.5, ALU.mult, ALU.add, [zs], [zs])
                tt("dve", zs.t[:, sl], zs.t[:, sl], pb.t[:, :], ALU.mult, [zs, pb], [zs])
                pb = dense_tm(xT, wap(C_GA + 512 * k, 512), 512)
                act(sga.t[:, sl], pb.t[:, :], AF.Tanh, [pb], [sga], scale=0.5)
                pb = dense_tm(xT, wap(C_GB + 512 * k, 512), 512)
                act(sgb.t[:, sl], pb.t[:, :], AF.Tanh, [pb], [sgb], scale=0.5)
            ts("pool", og.t[:, :], og.t[:, :], 0.5, 0.5, ALU.mult, ALU.add, [og], [og])
            ts("pool", sga.t[:, :], sga.t[:, :], 0.5, 0.5, ALU.mult, ALU.add, [sga], [sga])
            ts("pool", sgb.t[:, :], sgb.t[:, :], 0.5, 0.5, ALU.mult, ALU.add, [sgb], [sgb])

    def mlstm(pass_):
        qk, og, ha, qkT = A[0], A[1], A[5], A[7]
        li, lf = sm["li"], sm["lfgg"]
        if pass_ == 2:
            for half in range(2):
                pb = pget()
                for j in range(4):
                    c = half * 4 + j
                    tr(pb.t[:, j * P:(j + 1) * P], qk.t[:, c * P:(c + 1) * P], [qk], [pb])
                if half == 0:
                    amul(qkT.t[:, 0:512], pb.t[:, :], float(P ** -0.5), [pb], [qkT])
                else:
                    cp("dve", qkT.t[:, 512:1024], pb.t[:, :], [pb], [qkT])
        for h in range(HM):
            m = mw[h % 2]
            kTM = qk.t[:, 512 + h * P:512 + (h + 1) * P]
            if pass_ == 2:
                qT = qkT.t[:, h * P:(h + 1) * P]
                kT = qkT.t[:, 512 + h * P:512 + (h + 1) * P]
                Lp, DT, W, num, tI = m["Lp"], m["DT"], m["W"], m["num"], m["tI"]
                ts("pool", Lp.t[:, :], triLs, lf.t[:, h:h + 1], None, ALU.mult, None, [cst, lf], [Lp])
                stt("pool", Lp.t[:, :], ident, li.t[:, h:h + 1], Lp.t[:, :], ALU.mult, ALU.add, [cst, li, Lp], [Lp])
                pb = pget()
                mm(pb.t[:, 0:P], Lp.t[:, :], triU, True, True, [Lp, cst], [pb])
                mm(pb.t[:, P:2 * P], kT, qT, True, True, [qkT], [pb])
                act(DT.t[:, :], pb.t[:, 0:P], AF.Exp, [pb], [DT])
                tt("pool", DT.t[:, :], DT.t[:, :], triU, ALU.mult, [DT, cst], [DT])
                tt("dve", W.t[:, :], pb.t[:, P:2 * P], DT.t[:, :], ALU.mult, [pb, DT], [W])
                pa = pget()
                mm(pa.t[:, 0:257], W.t[:, :], vaug.t[:, h, :], True, True, [W, vaug], [pa])
                pc = pget()
                mm(pc.t[:, 0:257], qT, Caug.t[:, h, :], True, True, [qkT, Caug], [pc])
                amul(tI.t[:, :], pc.t[:, 0:257], sm["eg"].t[:, h:h + 1], [pc, sm["eg"]], [tI])
                tt("dve", num.t[:, :], pa.t[:, 0:257], tI.t[:, :], ALU.add, [pa, tI], [num])
                s1, s2, s3 = sm["s1"], sm["s2"], sm["s3"]
                ts("dve", s1.t[:, h:h + 1], num.t[:, 256:257], -1.0, None, ALU.mult, None, [num], [s1])
                tt("dve", s1.t[:, h:h + 1], s1.t[:, h:h + 1], num.t[:, 256:257], ALU.max, [s1, num], [s1])
                ts("dve", s1.t[:, h:h + 1], s1.t[:, h:h + 1], 1.0, None, ALU.max, None, [s1], [s1])
                sy.op("dve", lambda E, a=s1.t[:, h:h + 1]: E.reciprocal(a, a), [s1], [s1])
                sumsq_rows(num.t[:, 0:256], A[8], s2.t[:, h:h + 1], [num], [s2], 256)
                tt("dve", s3.t[:, h:h + 1], s1.t[:, h:h + 1], s1.t[:, h:h + 1], ALU.mult, [s1], [s3])
                stt("dve", s3.t[:, h:h + 1], s3.t[:, h:h + 1], 1.0 / 256, s2.t[:, h:h + 1], ALU.mult, ALU.mult, [s3, s2], [s3])
                rsqrt_op(s3.t[:, h:h + 1], s3.t[:, h:h + 1], 1.0, RMS_EPS, [s3], [s3])
                tt("dve", s3.t[:, h:h + 1], s3.t[:, h:h + 1], s1.t[:, h:h + 1], ALU.mult, [s3, s1], [s3])
                stt("dve", ha.t[:, h * 256:(h + 1) * 256], num.t[:, 0:256], s3.t[:, h:h + 1],
                    rp.t[:, R_GM + h * 256:R_GM + (h + 1) * 256], ALU.mult, ALU.mult, [num, s3, rp], [ha])
            kw_ = m["kw"]
            ts("pool", kw_.t[:, :], kTM, sm["wk"].t[:, h:h + 1], None, ALU.mult, None, [qk, sm["wk"]], [kw_])
            pd = pget()
            mm(pd.t[:, 0:257], kw_.t[:, :], vaug.t[:, h, :], True, True, [kw_, vaug], [pd])
            stt("dve", Caug.t[:, h, :], Caug.t[:, h, :], sm["ege"].t[:, h:h + 1], pd.t[:, 0:257], ALU.mult, ALU.add,
                [Caug, sm["ege"], pd], [Caug])
        if pass_ == 2:
            tt("pool", ha.t[:, :], ha.t[:, :], og.t[:, :], ALU.mult, [ha, og], [ha])

    def conv_halo_shift():
        cp("pool", ubuf.t[:, :, 0:3], ubuf.t[:, :, P:P + 3], [ubuf], [ubuf])

    def gdn(pass_):
        zs, hb = A[2], A[6]
        scr, scr2 = A[8], A[7]
        for part in range(3):
            c0 = part * 8
            if pass_ == 1 and part == 0:
                continue
            dst = cv.t[:, c0:c0 + 8, :]
            wv = cw.t[:, c0:c0 + 8, :]
            eng = "dve" if part % 2 == 0 else "pool"
            s3d = scr.t[:, :].rearrange("p (c n) -> p c n", c=8)
            for c in range(c0, c0 + 8):
                for k in range(4):
                    src = ubuf.t[:, c, k:k + P]
                    wsc = cw.t[:, c, k:k + 1]
                    if k == 0:
                        ts("pool", cv.t[:, c, :], src, wsc, None, ALU.mult, None, [ubuf, cw], [cv])
                    else:
                        stt("dve", cv.t[:, c, :], src, wsc, cv.t[:, c, :], ALU.mult, ALU.add, [ubuf, cw, cv], [cv])
            act(s3d, dst, AF.Tanh, [cv], [scr], scale=0.5)
            ts(eng, s3d, s3d, 0.5, 0.5, ALU.mult, ALU.add, [scr], [scr])
            tt(eng, dst, dst, s3d, ALU.mult, [cv, scr], [cv])
            if part < 2:
                sy.op("act", lambda E, a=s3d, b=dst: E.square(a, b), [cv], [scr])
                for hf in range(2):
                    pb = pget()
                    mm(pb.t[:, :], ones, scr.t[:, hf * 512:(hf + 1) * 512], True, True, [cst, scr], [pb])
                    r2 = scr2.t[:, hf * 512:(hf + 1) * 512]
                    if part == 0:
                        rsqrt_op(r2, pb.t[:, :], float(P), float(P) * RMS_EPS, [pb], [scr2])
                    else:
                        rsqrt_op(r2, pb.t[:, :], 1.0, RMS_EPS, [pb], [scr2])
                tt("dve", dst, dst, scr2.t[:, :].rearrange("p (c n) -> p c n", c=8), ALU.mult, [cv, scr2], [cv])
        chk(20)
        conv_halo_shift()
        for g4 in range(4):
            pb = pget()
            for j in range(4):
                c = 8 + g4 * 4 + j
                tr(pb.t[:, j * P:(j + 1) * P], cv.t[:, c, :], [cv], [pb])
            src = pb.t[:, :].rearrange("p (c n) -> p c n", c=4)
            if g4 % 2 == 0:
                acopy(kvtm.t[:, g4 * 4:g4 * 4 + 4, :], src, [pb], [kvtm])
            else:
                cp("dve", kvtm.t[:, g4 * 4:g4 * 4 + 4, :], src, [pb], [kvtm])
        gg = sm["lfgg"]
        chk(21)
        for h in range(HG):
            w = gw[h % NSLOT]
            qT = cv.t[:, h, :]
            kT = cv.t[:, 8 + h, :]
            kTM = kvtm.t[:, h, :]
            vTM = kvtm.t[:, 8 + h, :]
            Lg, Dgs, DgT = w["Lg"], w["Dgs"], w["DgT"]
            ts("pool", Lg.t[:, :], triLs, gg.t[:, 4 + h:5 + h], None, ALU.mult, None, [cst, gg], [Lg])
            pb = pget()
            pbD = pget()
            mm(pb.t[:, 0:P], triU, Lg.t[:, :], True, True, [cst, Lg], [pb])
            mm(pbD.t[:, P:2 * P], kT, kT, True, True, [cv], [pbD])
            if pass_ == 2:
                mm(pb.t[:, 2 * P:3 * P], Lg.t[:, :], triU, True, True, [Lg, cst], [pb])
                mm(pbD.t[:, 3 * P:4 * P], kT, qT, True, True, [cv], [pbD])
            act(Dgs.t[:, :], pb.t[:, 0:P], AF.Exp, [pb], [Dgs])
            tt("pool", Dgs.t[:, :], Dgs.t[:, :], triLs, ALU.mult, [Dgs, cst], [Dgs])
            Q, Pm, R = [w["Q0"], w["Q1"]], [w["P0"], w["P1"]], [w["R0"], w["R1"]]
            stt("dve", Q[0].t[:, :], pbD.t[:, P:2 * P], sm["nbeta"].t[:, h:h + 1], Dgs.t[:, :], ALU.mult, ALU.mult,
                [pbD, sm["nbeta"], Dgs], [Q[0]])
            if pass_ == 2:
                act(DgT.t[:, :], pb.t[:, 2 * P:3 * P], AF.Exp, [pb], [DgT])
                tt("pool", DgT.t[:, :], DgT.t[:, :], triU, ALU.mult, [DgT, cst], [DgT])
                tt("dve", w["attnT"].t[:, :], pbD.t[:, 3 * P:4 * P], DgT.t[:, :], ALU.mult, [pbD, DgT], [w["attnT"]])
            chk(22)
            p2 = pget()
            tr(p2.t[:, 0:P], Q[0].t[:, :], [Q[0]], [p2])
            cp("dve", Pm[0].t[:, :], p2.t[:, 0:P], [p2], [Pm[0]])
            tt("dve", R[0].t[:, :], p2.t[:, 0:P], ident, ALU.add, [p2, cst], [R[0]])
            cur = 0
            for k in range(1, 7):
                nxt = 1 - cur
                p3 = pget()
                mm(p3.t[:, 0:P], Pm[cur].t[:, :], Q[cur].t[:, :], True, True, [Pm[cur], Q[cur]], [p3])
                acopy(Q[nxt].t[:, :], p3.t[:, 0:P], [p3], [Q[nxt]])
                if k < 6:
                    p3b = pget()
                    mm(p3b.t[:, 0:P], Q[cur].t[:, :], Pm[cur].t[:, :], True, True, [Pm[cur], Q[cur]], [p3b])
                    cp("dve", Pm[nxt].t[:, :], p3b.t[:, 0:P], [p3b], [Pm[nxt]])
                p3c = pget()
                mm(p3c.t[:, 0:P], Q[nxt].t[:, :], R[cur].t[:, :], True, True, [Q[nxt], R[cur]], [p3c])
                tt("dve", R[nxt].t[:, :], R[cur].t[:, :], p3c.t[:, 0:P], ALU.add, [R[cur], p3c], [R[nxt]])
                cur = nxt
            chk(23)
            TT = R[cur]
            bv, bkg, kd = w["bv"], w["bkg"], w["kd"]
            ts("pool", bv.t[:, :], vTM, sm["beta"].t[:, h:h + 1], None, ALU.mult, None, [kvtm, sm["beta"]], [bv])
            ts("pool", bkg.t[:, :], kTM, sm["bg"].t[:, h:h + 1], None, ALU.mult, None, [kvtm, sm["bg"]], [bkg])
            ts("pool", kd.t[:, :], kTM, sm["ekd"].t[:, h:h + 1], None, ALU.mult, None, [kvtm, sm["ekd"]], [kd])
            p4 = pget()
            p4b = pget()
            mm(p4.t[:, 0:P], TT.t[:, :], bv.t[:, :], True, True, [TT, bv], [p4])
            mm(p4b.t[:, 0:P], bkg.t[:, :], TT.t[:, :], True, True, [TT, bkg], [p4b])
            u, wT, vn = w["u"], w["wT"], w["vn"]
            acopy(wT.t[:, :], p4b.t[:, 0:P], [p4b], [wT])
            chk(24)
            egh = sm["ege"].t[:, 4 + h:5 + h]
            if pass_ == 2:
                S = SX.t[:, h, 0:P]
                p5 = pget()
                mm(p5.t[:, 0:P], wT.t[:, :], S, True, True, [wT, SX], [p5])
                p5b = pget()
                mm(p5b.t[:, 0:P], qT, S, True, True, [cv, SX], [p5b])
                cp("dve", u.t[:, 0:P], p4.t[:, 0:P], [p4], [u])
                tt("dve", vn.t[:, 0:P], u.t[:, 0:P], p5.t[:, 0:P], ALU.subtract, [u, p5], [vn])
                amul(w["t2"].t[:, :], p5b.t[:, 0:P], sm["eg"].t[:, 4 + h:5 + h], [p5b, sm["eg"]], [w["t2"]])
                mm(p5.t[:, 2 * P:3 * P], w["attnT"].t[:, :], vn.t[:, 0:P], True, True, [w["attnT"], vn], [p5])
                mm(p5.t[:, 3 * P:4 * P], kd.t[:, :], vn.t[:, 0:P], True, True, [kd, vn], [p5])
                tt("dve", hb.t[:, h * P:(h + 1) * P], p5.t[:, 2 * P:3 * P], w["t2"].t[:, :], ALU.add, [p5, w["t2"]], [hb])
                stt("dve", S, S, egh, p5.t[:, 3 * P:4 * P], ALU.mult, ALU.add, [SX, sm["ege"], p5], [SX])
            else:
                X = SX.t[:, h, :]
                cp("dve", u.t[:, 0:P], p4.t[:, 0:P], [p4], [u])
                p5 = pget()
                mm(p5.t[:, 0:256], wT.t[:, :], X, True, True, [wT, SX], [p5])
                tt("dve", vn.t[:, :], u.t[:, :], p5.t[:, 0:256], ALU.subtract, [u, p5], [vn])
                mm(p5.t[:, 256:512], kd.t[:, :], vn.t[:, :], True, True, [kd, vn], [p5])
                stt("dve", X, X, egh, p5.t[:, 256:512], ALU.mult, ALU.add, [SX, sm["ege"], p5], [SX])
        if pass_ == 2:
            s5, s6 = sm["s5"], sm["s6"]
            sy.op("act", lambda E: E.square(scr.t[:, :], hb.t[:, :]), [hb], [scr])
            for h in range(HG):
                sy.op("dve", lambda E, h=h: E.reduce_sum(s5.t[:, h:h + 1], scr.t[:, h * P:(h + 1) * P], AX.X), [scr], [s5])
            rsqrt_op(s6.t[:, 0:8], s5.t[:, 0:8], 1.0 / P, RMS_EPS, [s5], [s6])
            for h in range(HG):
                stt("pool" if h % 2 else "dve", hb.t[:, h * P:(h + 1) * P], hb.t[:, h * P:(h + 1) * P], s6.t[:, h:h + 1],
                    rp.t[:, R_GG:R_GG + P], ALU.mult, ALU.mult, [hb, s6, rp], [hb])
            tt("pool", hb.t[:, :], hb.t[:, :], zs.t[:, :], ALU.mult, [hb, zs], [hb])

    def merge_ffn(l, xtb, out_b, out_ap):
        y, z, z2 = A[0], A[1], A[2]
        sga, sgb, ha, hb = A[3], A[4], A[5], A[6]
        scr = A[8]
        transpose_tm(ha, fmA)
        transpose_tm(hb, fmB)
        for k in range(2):
            sl = slice(512 * k, 512 * (k + 1))
            pa = dense_tm(fmA, w_ba_d[l][:, sl], 512)
            pb = dense_tm(fmB, w_bb_d[l][:, sl], 512)
            tt("dve", y.t[:, sl], pa.t[:, :], sga.t[:, sl], ALU.mult, [pa, sga], [y])
            tt("dve", scr.t[:, sl], pb.t[:, :], sgb.t[:, sl], ALU.mult, [pb, sgb], [scr])
            tt("pool", y.t[:, sl], y.t[:, sl], scr.t[:, sl], ALU.add, [y, scr], [y])
        transpose_tm(y, fmA)
        for k in range(2):
            sl = slice(512 * k, 512 * (k + 1))
            pa = dense_tm(fmA, w_out_d[l][:, sl], 512)
            stt("dve", z.t[:, sl], xtb.t[:, sl], float(ALPHA), pa.t[:, :], ALU.mult, ALU.add, [xtb, pa], [z])
        layernorm(z, R_L1G, R_L1B, z, scr)
        transpose_tm(z, fmB)
        actb = cv
        nblk = DFF // P
        for k in range(6):
            nb = min(4, nblk - 4 * k)
            ncol = nb * P
            pg = dense_fm(fmB, w_up_d[l][:, 512 * k:512 * k + ncol], ncol)
            pu = dense_fm(fmB, w_up_d[l][:, DFF + 512 * k:DFF + 512 * k + ncol], ncol)
            sc = scr.t[:, 0:ncol]
            act(sc, pg.t[:, 0:ncol], AF.Tanh, [pg], [scr], scale=0.5)
            ts("pool", sc, sc, 0.5, 0.5, ALU.mult, ALU.add, [scr], [scr])
            tt("dve", sc, sc, pg.t[:, 0:ncol], ALU.mult, [scr, pg], [scr])
            tt("dve", actb.t[:, 4 * k:4 * k + nb, :], sc.rearrange("p (c n) -> p c n", c=nb),
               pu.t[:, 0:ncol].rearrange("p (c n) -> p c n", c=nb), ALU.mult, [scr, pu], [actb])
        pd = [pget(), pget()]
        for k in range(6):
            nk = min(4, nblk - 4 * k)
            s, sv = load_slab(w_dn_d[l][512 * k:512 * k + nk * P, :], D, nk=nk)
            for kk in range(nk):
                kf = 4 * k + kk
                for hf in range(2):
                    mm(pd[hf].t[:, :], actb.t[:, kf, :], sv[:, kk, hf * 512:(hf + 1) * 512], kf == 0, kf == nblk - 1,
                       [actb, s], [pd[hf]])
        for hf in range(2):
            sl = slice(512 * hf, 512 * (hf + 1))
            stt("dve", z2.t[:, sl], z.t[:, sl], float(ALPHA), pd[hf].t[:, :], ALU.mult, ALU.add, [z, pd[hf]], [z2])
        layernorm(z2, R_L2G, R_L2B, z2, scr)
        sy.dma("sp", out_ap, z2.t[:, :], reads=[z2], writes=[out_b])

    try:
        for l in range(L):
            if l == 0:
                xin_ap, xin_bufs = x_d, xin_b
            else:
                xin_ap, xin_bufs = xb_d[(l - 1) % 2], xb_b[(l - 1) % 2]
            if l == L - 1:
                xout_ap, xout_bufs = y_d, y_b
            else:
                xout_ap, xout_bufs = xb_d[l % 2], xb_b[l % 2]
            sy.dma("sp", rp.t[:, :], rowp_d[l], reads=[cdram], writes=[rp])
            sy.dma("sp", cw.t[:, :, :], convw_d[l].rearrange("p (c k) -> p c k", k=4), reads=[cdram], writes=[cw])
            act(nexpA.t[:, :], rp.t[:, R_AL:R_AL + 8], AF.Exp, [rp], [nexpA])
            ts("dve", nexpA.t[:, :], nexpA.t[:, :], -1.0, None, ALU.mult, None, [nexpA], [nexpA])
            chk(1)
            hin_b, hout_b = Buf(f"hin{l}", None), Buf(f"hout{l}", None)
            sy.dma("pool", hin_d[l][:, :], xin_ap[T - P:T, :], reads=[xin_bufs[NT - 1]], writes=[hin_b])
            sy.collective("AllGather", groups, hin_d[l].ap().opt(), hout_d[l].ap().opt(), reads=[hin_b], writes=[hout_b])
            hx, hc = xt[0], A[8]
            mset("pool", hx.t[:, :], 0.0, [hx])
            for ci, c in enumerate(CANDS):
                sy.dma("sp", hc.t[:, :], hout_d[l][c * P:(c + 1) * P, :], reads=[hout_b], writes=[hc])
                stt("dve", hx.t[:, :], hc.t[:, :], cmask.t[:, 6 + ci:7 + ci], hx.t[:, :], ALU.mult, ALU.add, [hc, cmask, hx], [hx])
            chk(2)
            transpose_tm(hx, fmA)
            chk(3)
            inproj(l, fmA, 1, halo_only=True)
            chk(4)
            conv_halo_shift()
            cp("pool", halo0.t[:, :, :], ubuf.t[:, :, 0:3], [ubuf], [halo0])
            mset("pool", Caug.t[:, :, :], 0.0, [Caug])
            mset("pool", gseg.t[:, :], 0.0, [gseg])
            mset("pool", vaug.t[:, :, 256:257], 1.0, [vaug])
            mset("pool", SX.t[:, :, 0:P], 0.0, [SX])
            for h in range(HG):
                cp("pool", SX.t[:, h, P:2 * P], ident, [cst], [SX])
            for s in range(NSLOT):
                mset("pool", gw[s]["u"].t[:, :], 0.0, [gw[s]["u"]])
            for i in range(NT):
                xtb = xt[i % 2]
                sy.dma("sp", xtb.t[:, :], xin_ap[i * P:(i + 1) * P, :], reads=[xin_bufs[i]], writes=[xtb])
                transpose_tm(xtb, fmA)
                inproj(l, fmA, 1)
                chk(5)
                gates(1)
                chk(6)
                mlstm(1)
                chk(7)
                gdn(1)
                chk(8)
            sgi_b, sgo_b, smi_b, smo_b = (Buf(n, None) for n in ("sgi", "sgo", "smi", "smo"))
            sy.dma("pool", sgi_d[l][:, :].rearrange("(h p) n -> p h n", p=P), SX.t[:, :, :], reads=[SX], writes=[sgi_b])
            sy.collective("AllGather", groups, sgi_d[l].ap().opt(), sgo_d[l].ap().opt(), reads=[sgi_b], writes=[sgo_b])
            eG = eGt
            mset("pool", eG.t[:, :], 0.0, [eG])
            act(eG.t[:, 0:4], gseg.t[:, 0:4], AF.Exp, [gseg], [eG])
            sy.dma("pool", smi_d[l][0:4 * P, :].rearrange("(h p) n -> p h n", p=P), Caug.t[:, :, :], reads=[Caug], writes=[smi_b])
            sy.dma("pool", smi_d[l][4 * P:5 * P, :], eG.t[:, :], reads=[eG], writes=[smi_b])
            sy.collective("AllGather", groups, smi_d[l].ap().opt(), smo_d[l].ap().opt(), reads=[smi_b], writes=[smo_b])
            chk(9)
            mset("pool", SX.t[:, :, 0:P], 0.0, [SX])
            mset("pool", Caug.t[:, :, :], 0.0, [Caug])
            for ci, c in enumerate(CANDS):
                mc = cmask.t[:, ci:ci + 1]
                for hf in range(2):
                    Xc = A[hf]
                    sy.dma("sp", Xc.t[:, :].rearrange("p (h n) -> p h n", h=4),
                           sgo_d[l][(c * 8 + hf * 4) * P:(c * 8 + hf * 4 + 4) * P, :].rearrange("(h p) n -> p h n", p=P),
                           reads=[sgo_b], writes=[Xc])
                    for hh in range(4):
                        h = hf * 4 + hh
                        Lc = Xc.t[:, hh * 256:hh * 256 + P]
                        Pc = Xc.t[:, hh * 256 + P:hh * 256 + 2 * P]
                        w = gw[h % NSLOT]
                        S = SX.t[:, h, 0:P]
                        pb = pget()
                        tr(pb.t[:, 0:P], Pc, [Xc], [pb])
                        acopy(w["P0"].t[:, :], pb.t[:, 0:P], [pb], [w["P0"]])
                        mm(pb.t[:, P:2 * P], w["P0"].t[:, :], S, True, True, [w["P0"], SX], [pb])
                        tt("dve", w["Q0"].t[:, :], pb.t[:, P:2 * P], Lc, ALU.add, [pb, Xc], [w["Q0"]])
                        tt("dve", w["Q0"].t[:, :], w["Q0"].t[:, :], S, ALU.subtract, [w["Q0"], SX], [w["Q0"]])
                        stt("dve", S, w["Q0"].t[:, :], mc, S, ALU.mult, ALU.add, [w["Q0"], cmask, SX], [SX])
                Cc, Ec = Ccb, sm["s2"]
                sy.dma("sp", Cc.t[:, :, :],
                       smo_d[l][c * 5 * P:(c * 5 + 4) * P, :].rearrange("(h p) n -> p h n", p=P), reads=[smo_b], writes=[Cc])
                sy.dma("sp", Ec.t[:, 0:4], smo_d[l][(c * 5 + 4) * P:(c * 5 + 5) * P, 0:4], reads=[smo_b], writes=[Ec])
                ts("dve", Ec.t[:, 0:4], Ec.t[:, 0:4], -1.0, None, ALU.add, None, [Ec], [Ec])
                ts("dve", Ec.t[:, 0:4], Ec.t[:, 0:4], mc, 1.0, ALU.mult, ALU.add, [Ec, cmask], [Ec])
                for h in range(HM):
                    ts("dve", Caug.t[:, h, :], Caug.t[:, h, :], Ec.t[:, h:h + 1], None, ALU.mult, None, [Caug, Ec], [Caug])
                    stt("dve", Caug.t[:, h, :], Cc.t[:, h, :], mc, Caug.t[:, h, :], ALU.mult, ALU.add,
                        [Cc, cmask, Caug], [Caug])
            cp("pool", ubuf.t[:, :, 0:3], halo0.t[:, :, :], [halo0], [ubuf])
            chk(10)
            for i in range(NT):
                xtb = xt[i % 2]
                sy.dma("sp", xtb.t[:, :], xin_ap[i * P:(i + 1) * P, :], reads=[xin_bufs[i]], writes=[xtb])
                transpose_tm(xtb, fmA)
                inproj(l, fmA, 2)
                gates(2)
                chk(11)
                mlstm(2)
                chk(12)
                gdn(2)
                chk(13)
                merge_ffn(l, xtb, xout_bufs[i], xout_ap[i * P:(i + 1) * P, :])
    except _Stop:
        pass
    sy.final_wait("sp", y_b)
    sy.emit()
    stack.close()
    return nc


def host_prep(inputs, L=DEPTH):
    f = np.float32
    rowp = np.zeros((L, NROW), f)
    rowp[:, R_GM:R_GM + 1024] = inputs["g_mlstm_norm"][:L]
    rowp[:, R_GG:R_GG + 128] = inputs["g_gdn_norm"][:L]
    rowp[:, R_L1G:R_L1G + 1024] = inputs["ln1_g"][:L]
    rowp[:, R_L1B:R_L1B + 1024] = inputs["ln1_b"][:L]
    rowp[:, R_L2G:R_L2G + 1024] = inputs["ln2_g"][:L]
    rowp[:, R_L2B:R_L2B + 1024] = inputs["ln2_b"][:L]
    rowp[:, R_GB:R_GB + 4] = inputs["b_igate"][:L]
    rowp[:, R_GB + 4:R_GB + 8] = inputs["b_fgate"][:L]
    rowp[:, R_GB + 16:R_GB + 24] = inputs["dt_bias"][:L]
    rowp[:, R_AL:R_AL + 8] = inputs["a_log"][:L]
    rowp = np.ascontiguousarray(np.broadcast_to(rowp[:, None, :], (L, P, NROW)))
    cwt = np.asarray(inputs["conv_w"][:L], f)
    convw = np.ascontiguousarray(cwt.transpose(0, 2, 1).reshape(L, 24, P, 4).transpose(0, 2, 1, 3).reshape(L, P, 96))
    idx = np.arange(P)
    ident = np.eye(P, dtype=f)
    ones = np.ones((P, P), f)
    triU = (idx[:, None] <= idx[None, :]).astype(f)
    triLs = (idx[:, None] > idx[None, :]).astype(f)
    consts = np.ascontiguousarray(np.concatenate([ident, ones, triU, triLs], axis=1))
    return rowp, convw, consts


def run(inputs, NT, L):
    T = NT * P
    x = np.asarray(inputs["x"], np.float32)
    rowp, convw, consts = host_prep(inputs, L)
    w_in_np = np.asarray(inputs["w_in"], np.float32)
    w_g = np.zeros((L, D, P), np.float32)
    w_g[:, :, 0:8] = w_in_np[:L, :, C_IF:C_IF + 8]
    w_g[:, :, 8:24] = w_in_np[:L, :, C_BG:C_BG + 16]
    shared = {
        "w_in": np.ascontiguousarray(inputs["w_in"][:L], dtype=np.float32),
        "w_g": w_g,
        "w_ba": np.ascontiguousarray(inputs["w_branch_a"][:L], dtype=np.float32),
        "w_bb": np.ascontiguousarray(inputs["w_branch_b"][:L], dtype=np.float32),
        "w_out": np.ascontiguousarray(inputs["w_out"][:L], dtype=np.float32),
        "w_up": np.ascontiguousarray(inputs["w_ffn_up"][:L], dtype=np.float32),
        "w_dn": np.ascontiguousarray(inputs["w_ffn_down"][:L], dtype=np.float32),
        "rowp": rowp, "convw": convw, "consts": consts,
    }
    in_maps = []
    for r in range(NCORES):
        b, j = r // GRP, r % GRP
        cm = np.zeros((P, 12), np.float32)
        for ci, c in enumerate([0, 1, 2, 4, 5, 6]):
            cm[:, ci] = 1.0 if (c // GRP == b and c % GRP < j) else 0.0
            cm[:, 6 + ci] = 1.0 if (c // GRP == b and c % GRP == j - 1) else 0.0
        m = dict(shared)
        m["x"] = np.ascontiguousarray(x[b, j * T:(j + 1) * T, :])
        m["cmask"] = cm
        in_maps.append(m)
    nc = build(NT, L)
    res = run_bass_kernel_spmd(nc, in_maps, core_ids=list(range(NCORES)))
    out = np.zeros((BATCH, GRP * T, D), np.float32)
    for r in range(NCORES):
        b, j = r // GRP, r % GRP
        out[b, j * T:(j + 1) * T, :] = res.results[r]["y"]
    return out


def kernel(**inputs):
    return run(inputs, SEQ // GRP // P, DEPTH)
```
